# Optimizing a Trainium2 kernel written in Bass

```python
import math, functools
import jax, jax.numpy as jnp
from jax import lax
import numpy as np

D_MODEL = 2048
BATCH = 4
SEQ = 4096
DEPTH = 2

DN_HEADS = 8
DN_HEAD_DIM = 128
DN_WIDTH = DN_HEADS * DN_HEAD_DIM
DN_CONV = 4
DN_CHUNK = 64
SG_GROUPS = 8
SG_GROUP_DIM = 128
SG_WIDTH = SG_GROUPS * SG_GROUP_DIM
SG_CHUNK = 128
D_FF = 5632
FFN_CONV = 3
EPS = 1e-6

SPLIT_SIZES = (3 * DN_WIDTH, DN_WIDTH, DN_HEADS, DN_HEADS, SG_WIDTH, SG_WIDTH, D_MODEL, D_MODEL)
SPLIT_POINTS = tuple(int(s) for s in np.cumsum(SPLIT_SIZES)[:-1])
IN_COLS = int(sum(SPLIT_SIZES))

kernel_name = "hybrid_deltanet_gmlp_convffn_gated_merge"


def rmsnorm(x, g):
    xf = x.astype(jnp.float32)
    y = xf * lax.rsqrt(jnp.mean(xf * xf, axis=-1, keepdims=True) + EPS)
    return (y * g.astype(jnp.float32)).astype(x.dtype)


def layernorm(x, g, b):
    xf = x.astype(jnp.float32)
    mu = jnp.mean(xf, axis=-1, keepdims=True)
    xc = xf - mu
    y = xc * lax.rsqrt(jnp.mean(xc * xc, axis=-1, keepdims=True) + EPS)
    return (y * g.astype(jnp.float32) + b.astype(jnp.float32)).astype(x.dtype)


def l2norm(x):
    xf = x.astype(jnp.float32)
    return xf * lax.rsqrt(jnp.sum(xf * xf, axis=-1, keepdims=True) + EPS)


def causal_dwconv(x, w):
    K = w.shape[0]
    T = x.shape[1]
    xp = jnp.pad(x, ((0, 0), (K - 1, 0), (0, 0)))
    out = xp[:, 0:T] * w[0]
    for j in range(1, K):
        out = out + xp[:, j:j + T] * w[j]
    return out


def gated_delta_rule_chunked(q, k, v, g, beta):
    B, T, H, Dk = q.shape
    Dv = v.shape[-1]
    C = DN_CHUNK
    N = T // C

    def to_chunks(t):
        t = t.astype(jnp.float32).reshape((B, N, C, H) + t.shape[3:])
        return jnp.moveaxis(t, 3, 1)

    q = to_chunks(q) * (Dk ** -0.5)
    k = to_chunks(k)
    v = to_chunks(v)
    g = jnp.cumsum(to_chunks(g), axis=-1)
    beta = to_chunks(beta)
    k_beta = k * beta[..., None]
    v_beta = v * beta[..., None]

    causal = jnp.tril(jnp.ones((C, C), dtype=bool))
    strict = jnp.tril(jnp.ones((C, C), dtype=bool), -1)
    decay = jnp.exp(jnp.where(causal, g[..., :, None] - g[..., None, :], -jnp.inf))

    L = jnp.where(strict, jnp.einsum('bhnid,bhnjd->bhnij', k_beta, k) * decay, 0.0)
    eye = jnp.eye(C, dtype=jnp.float32)
    Tinv = lax.linalg.triangular_solve(L + eye, jnp.broadcast_to(eye, L.shape),
                                       left_side=True, lower=True, unit_diagonal=True)
    u = jnp.einsum('bhnij,bhnjv->bhniv', Tinv, v_beta)
    w = jnp.einsum('bhnij,bhnjk->bhnik', Tinv, k_beta * jnp.exp(g)[..., None])

    attn = jnp.where(causal, jnp.einsum('bhnid,bhnjd->bhnij', q, k) * decay, 0.0)
    q_dec = q * jnp.exp(g)[..., None]
    g_last = g[..., -1]
    k_dec = k * jnp.exp(g_last[..., None] - g)[..., None]

    xs = tuple(jnp.moveaxis(t, 2, 0) for t in (u, w, attn, q_dec, k_dec, g_last))

    def step(S, inp):
        u_n, w_n, a_n, qd_n, kd_n, gl_n = inp
        v_new = u_n - jnp.einsum('bhck,bhkv->bhcv', w_n, S)
        o_n = (jnp.einsum('bhck,bhkv->bhcv', qd_n, S)
               + jnp.einsum('bhij,bhjv->bhiv', a_n, v_new))
        S = S * jnp.exp(gl_n)[..., None, None] + jnp.einsum('bhck,bhcv->bhkv', kd_n, v_new)
        return S, o_n

    S0 = jnp.zeros((B, H, Dk, Dv), jnp.float32)
    _, o = lax.scan(step, S0, xs)
    return jnp.transpose(o, (1, 0, 3, 2, 4)).reshape(B, T, H, Dv)


def setup_inputs(seed: int = 0) -> dict:
    key = jax.random.key(seed)
    ks = jax.random.split(key, 24)
    f32 = jnp.float32

    def nrm(k, shape, scale):
        return jax.random.normal(k, shape, f32) * scale

    x = jax.random.normal(ks[0], (BATCH, SEQ, D_MODEL), f32)
    norm1_g = 1.0 + nrm(ks[1], (DEPTH, D_MODEL), 0.02)
    w_in = nrm(ks[2], (DEPTH, D_MODEL, IN_COLS), D_MODEL ** -0.5)
    dn_conv_w = nrm(ks[3], (DEPTH, DN_CONV, 3 * DN_WIDTH), DN_CONV ** -0.5)
    dn_a_log = jnp.log(jax.random.uniform(ks[4], (DEPTH, DN_HEADS), f32, 1.0, 16.0))
    dt = jnp.exp(jax.random.uniform(ks[5], (DEPTH, DN_HEADS), f32,
                                    math.log(0.001), math.log(0.1)))
    dn_dt_bias = dt + jnp.log(-jnp.expm1(-dt))
    dn_onorm_g = 1.0 + nrm(ks[6], (DEPTH, DN_HEAD_DIM), 0.02)
    sg_ln_g = 1.0 + nrm(ks[7], (DEPTH, SG_WIDTH), 0.02)
    sg_ln_b = nrm(ks[8], (DEPTH, SG_WIDTH), 0.02)
    sg_w = nrm(ks[9], (DEPTH, SG_GROUPS, SG_CHUNK, SG_CHUNK), 0.5 * SG_CHUNK ** -0.5)
    sg_b = 1.0 + nrm(ks[10], (DEPTH, SG_GROUPS, SG_CHUNK), 0.02)
    w_branch_a = nrm(ks[11], (DEPTH, DN_WIDTH, D_MODEL), DN_WIDTH ** -0.5)
    w_branch_b = nrm(ks[12], (DEPTH, SG_WIDTH, D_MODEL), SG_WIDTH ** -0.5)
    w_out = nrm(ks[13], (DEPTH, D_MODEL, D_MODEL), D_MODEL ** -0.5)
    norm2_g = 1.0 + nrm(ks[14], (DEPTH, D_MODEL), 0.02)
    ffn_w_gate = nrm(ks[15], (DEPTH, D_MODEL, D_FF), D_MODEL ** -0.5)
    ffn_w_up = nrm(ks[16], (DEPTH, D_MODEL, D_FF), D_MODEL ** -0.5)
    ffn_conv_w = nrm(ks[17], (DEPTH, FFN_CONV, D_FF), FFN_CONV ** -0.5)
    ffn_conv_b = nrm(ks[18], (DEPTH, D_FF), 0.02)
    ffn_w_down = nrm(ks[19], (DEPTH, D_FF, D_MODEL), D_FF ** -0.5)
    final_norm_g = 1.0 + nrm(ks[20], (D_MODEL,), 0.02)
    return {"x": x, "norm1_g": norm1_g, "w_in": w_in, "dn_conv_w": dn_conv_w,
            "dn_a_log": dn_a_log, "dn_dt_bias": dn_dt_bias, "dn_onorm_g": dn_onorm_g,
            "sg_ln_g": sg_ln_g, "sg_ln_b": sg_ln_b, "sg_w": sg_w, "sg_b": sg_b,
            "w_branch_a": w_branch_a, "w_branch_b": w_branch_b, "w_out": w_out,
            "norm2_g": norm2_g, "ffn_w_gate": ffn_w_gate, "ffn_w_up": ffn_w_up,
            "ffn_conv_w": ffn_conv_w, "ffn_conv_b": ffn_conv_b, "ffn_w_down": ffn_w_down,
            "final_norm_g": final_norm_g}


def reference(x, norm1_g, w_in, dn_conv_w, dn_a_log, dn_dt_bias, dn_onorm_g,
              sg_ln_g, sg_ln_b, sg_w, sg_b, w_branch_a, w_branch_b, w_out,
              norm2_g, ffn_w_gate, ffn_w_up, ffn_conv_w, ffn_conv_b, ffn_w_down,
              final_norm_g):
    B, T, _ = x.shape
    sg_mask = jnp.tril(jnp.ones((SG_CHUNK, SG_CHUNK), dtype=bool))
    for l in range(DEPTH):
        h = rmsnorm(x, norm1_g[l])
        proj = h @ w_in[l]
        qkv, z, b_raw, a_raw, u_raw, v_raw, ga_raw, gb_raw = jnp.split(proj, SPLIT_POINTS, axis=-1)

        qkv = jax.nn.silu(causal_dwconv(qkv, dn_conv_w[l]))
        q, k, v = jnp.split(qkv, 3, axis=-1)
        q = l2norm(q.reshape(B, T, DN_HEADS, DN_HEAD_DIM))
        k = l2norm(k.reshape(B, T, DN_HEADS, DN_HEAD_DIM))
        v = v.reshape(B, T, DN_HEADS, DN_HEAD_DIM)
        beta = jax.nn.sigmoid(b_raw.astype(jnp.float32))
        g = -jnp.exp(dn_a_log[l].astype(jnp.float32)) * jax.nn.softplus(
            a_raw.astype(jnp.float32) + dn_dt_bias[l].astype(jnp.float32))
        o = gated_delta_rule_chunked(q, k, v, g, beta)
        o = rmsnorm(o, dn_onorm_g[l]) * jax.nn.silu(
            z.reshape(B, T, DN_HEADS, DN_HEAD_DIM).astype(jnp.float32))
        y_a = o.reshape(B, T, DN_WIDTH).astype(x.dtype)

        u = jax.nn.gelu(u_raw, approximate=False)
        vg = layernorm(jax.nn.gelu(v_raw, approximate=False), sg_ln_g[l], sg_ln_b[l])
        vg = vg.reshape(B, T // SG_CHUNK, SG_CHUNK, SG_GROUPS, SG_GROUP_DIM)
        ws = jnp.where(sg_mask, sg_w[l], 0.0)
        mixed = (jnp.einsum('gij,bnjgc->bnigc', ws, vg)
                 + jnp.transpose(sg_b[l])[None, None, :, :, None])
        y_b = u * mixed.reshape(B, T, SG_WIDTH)

        merged = (jax.nn.sigmoid(ga_raw) * (y_a @ w_branch_a[l])
                  + jax.nn.sigmoid(gb_raw) * (y_b @ w_branch_b[l]))
        x = x + merged @ w_out[l]

        h2 = rmsnorm(x, norm2_g[l])
        gate = causal_dwconv(h2 @ ffn_w_gate[l], ffn_conv_w[l]) + ffn_conv_b[l]
        x = x + (jax.nn.silu(gate) * (h2 @ ffn_w_up[l])) @ ffn_w_down[l]
    return rmsnorm(x, final_norm_g)
```

```python
import contextlib
import numpy as np
import concourse.bass as bass
import concourse.mybir as mybir
from concourse.bass_utils import run_bass_kernel_spmd

F32 = mybir.dt.float32
BF16 = mybir.dt.bfloat16
AF = mybir.ActivationFunctionType
ALU = mybir.AluOpType

PE, ACT, DVE, POOL, SP = "tensor", "scalar", "vector", "gpsimd", "sync"
QUEUES = (PE, ACT, DVE, POOL, SP)

D = 2048
SEQ = 4096
BATCH = 4
DEPTH = 2
H = 8
DFF = 5632
INC = 10256
TG = 512
NB = TG // 128
KC = D // 128
NFT = DFF // 128
EPS = 1e-6
C_QKV, C_Z, C_B, C_A, C_U, C_V, C_GA, C_GB = 0, 3072, 4096, 4104, 4112, 5136, 6160, 8208
WSLOT = 16 * 512
NSLOT = 3
WRN_ELEMS = 20480


class Buf:
    __slots__ = ("name", "last_w", "rd_q", "rd_dma", "wsem", "rsem", "wcnt", "rcnt", "excl")

    def __init__(self, name, excl=False):
        self.name = name
        self.excl = excl
        self.last_w = None
        self.rd_q = {}
        self.rd_dma = []
        self.wsem = None
        self.rsem = None
        self.wcnt = 0
        self.rcnt = 0


class Op:
    __slots__ = ("q", "fn", "deps", "signal", "sigval", "is_dma", "dsem", "dval")

    def __init__(self, q, fn, is_dma=False):
        self.q = q
        self.fn = fn
        self.deps = []
        self.signal = False
        self.sigval = 0
        self.is_dma = is_dma
        self.dsem = None
        self.dval = 0


class Prog:
    def __init__(self, nc):
        self.nc = nc
        self.qops = {q: [] for q in QUEUES}
        self.ctx = contextlib.ExitStack()
        self.bar = {}
        self.out_dmas = []
        self.live_dmas = []
        self.semtab = {}
        self.rec = None

    def new_sem(self, name):
        return self.ctx.enter_context(self.nc.semaphore(name))

    def _track(self, op, reads, writes):
        deps = op.deps
        q = op.q
        comp = not op.is_dma

        def same(d):
            return comp and (not d.is_dma) and d.q == q
        bd = self.bar.get(q)
        if bd:
            for d in bd:
                if not same(d):
                    deps.append(d)
            self.bar[q] = None
        for b in reads:
            lw = b.last_w
            if lw is not None and not (q == PE and same(lw)):
                deps.append(lw)
            if b.excl:
                for r in b.rd_q.values():
                    if not same(r):
                        deps.append(r)
        for b in writes:
            lw = b.last_w
            if lw is not None and not same(lw):
                deps.append(lw)
            for r in b.rd_q.values():
                if not same(r):
                    deps.append(r)
            deps.extend(b.rd_dma)
        for b in reads:
            if op.is_dma:
                b.rd_dma.append(op)
            else:
                b.rd_q[q] = op
        for b in writes:
            b.last_w = op
            b.rd_q = {}
            b.rd_dma = []

    def begin(self):
        assert self.rec is None
        self.rec = []

    def end(self):
        r = self.rec
        self.rec = None
        return r

    def play(self, *streams):
        streams = [x for x in streams if x]
        idx = [0] * len(streams)
        total = sum(len(x) for x in streams)
        for _ in range(total):
            best = min((i for i in range(len(streams)) if idx[i] < len(streams[i])),
                       key=lambda i: (idx[i] + 0.5) / len(streams[i]))
            streams[best][idx[best]]()
            idx[best] += 1

    def op(self, q, fn, reads=(), writes=()):
        if self.rec is not None:
            self.rec.append(lambda: self._op(q, fn, reads, writes))
            return None
        return self._op(q, fn, reads, writes)

    def _op(self, q, fn, reads=(), writes=()):
        o = Op(q, fn)
        self._track(o, reads, writes)
        self.qops[q].append(o)
        return o

    def dma(self, q, fn, reads=(), writes=(), sem_buf=None, sem_kind="w", weight=False):
        if self.rec is not None:
            self.rec.append(lambda: self._dma(q, fn, reads, writes, sem_buf, sem_kind, weight))
            return None
        return self._dma(q, fn, reads, writes, sem_buf, sem_kind, weight)

    def _dma(self, q, fn, reads=(), writes=(), sem_buf=None, sem_kind="w", weight=False):
        o = Op(q, fn, is_dma=True)
        self._track(o, reads, writes)
        key = ("w_" if sem_kind == "w" else "r_") + sem_buf.name
        ent = self.semtab.get(key)
        if ent is None:
            ent = [self.new_sem(key), 0]
            self.semtab[key] = ent
        ent[1] += 16
        o.dsem, o.dval = ent[0], ent[1]
        self.qops[q].append(o)
        if not weight:
            self.live_dmas.append(o)
        return o

    def barrier(self):
        deps = []
        for q in (PE, ACT, DVE):
            for o in reversed(self.qops[q]):
                if not o.is_dma:
                    deps.append(o)
                    break
        deps.extend(self.live_dmas)
        self.live_dmas = []
        for q in (PE, ACT, DVE, SP):
            self.bar[q] = list(deps)

    def emit(self):
        nc = self.nc
        for q in QUEUES:
            for o in self.qops[q]:
                for d in o.deps:
                    if not d.is_dma:
                        d.signal = True
        qsem = {}
        for q in (PE, ACT, DVE):
            cnt = 0
            for o in self.qops[q]:
                if o.signal:
                    cnt += 1
                    o.sigval = cnt
            qsem[q] = self.new_sem("q_" + q)
        finals = {}
        for o in self.out_dmas:
            k = id(o.dsem)
            if k not in finals or finals[k][1] < o.dval:
                finals[k] = (o.dsem, o.dval)
        with nc.Block() as block:
            def run(q):
                def body(eng):
                    waited = {}
                    for o in self.qops[q]:
                        need = {}
                        for d in o.deps:
                            if d.is_dma:
                                s, v = d.dsem, d.dval
                            else:
                                s, v = qsem[d.q], d.sigval
                            k = id(s)
                            if waited.get(k, 0) >= v:
                                continue
                            if k not in need or need[k][1] < v:
                                need[k] = (s, v)
                        for k, (s, v) in need.items():
                            eng.wait_ge(s, v)
                            waited[k] = v
                        ins = o.fn(eng)
                        if o.is_dma:
                            ins.then_inc(o.dsem, 16)
                        elif o.signal:
                            ins.then_inc(qsem[q], 1)
                    if q == SP:
                        for (s, v) in finals.values():
                            eng.wait_ge(s, v)
                return body
            block.tensor(run(PE))
            block.scalar(run(ACT))
            block.vector(run(DVE))
            block.gpsimd(run(POOL))
            block.sync(run(SP))


class Tl:
    __slots__ = ("ap", "b")

    def __init__(self, ap, b):
        self.ap = ap
        self.b = b


class _Stop(Exception):
    pass


STOP = [99]


def _chk(k):
    if STOP[0] <= k:
        raise _Stop()


def build_program(ntg=SEQ // TG, depth=DEPTH, dbg=False):
    nc = bass.Bass("TRN2", target_bir_lowering=False)
    P = Prog(nc)

    def din(name, shape):
        return nc.dram_tensor(name, list(shape), F32, kind="ExternalInput").ap()

    x_in = din("x", [SEQ, D])
    w_in = din("w_in", [DEPTH, D, INC])
    w_ba = din("w_branch_a", [DEPTH, 1024, D])
    w_bb = din("w_branch_b", [DEPTH, 1024, D])
    w_out = din("w_out", [DEPTH, D, D])
    w_gate = din("ffn_w_gate", [DEPTH, D, DFF])
    w_up = din("ffn_w_up", [DEPTH, D, DFF])
    w_down = din("ffn_w_down", [DEPTH, DFF, D])
    cmat_d = din("cmat", [128, 6, 128])
    sel_d = din("sel", [8, 8, 128])
    gn_d = din("gnorm", [2 * DEPTH + 1, 128, D])
    cw_d = din("cw", [128, DEPTH, 24, 4])
    fcw_d = din("fcw", [128, DEPTH, NFT, 3])
    fcb_d = din("fcb", [128, DEPTH, NFT])
    ong_d = din("ong", [128, DEPTH])
    alog_d = din("alog", [8, DEPTH])
    dtb_d = din("dtb", [8, DEPTH])
    lng_d = din("lng", [DEPTH, 128, 1024])
    lnb_d = din("lnb", [DEPTH, 128, 1024])
    wst_d = din("wst", [DEPTH, 128, 8, 128])
    sgb_d = din("sgb", [DEPTH, 128, 8, 128])
    out_d = nc.dram_tensor("out", [SEQ, D], F32, kind="ExternalOutput").ap()
    dbg_d = None
    if dbg:
        dbg_d = nc.dram_tensor("dbg", [16, 128, 512], F32, kind="ExternalOutput").ap()

    es = P.ctx

    def sb(name, shape, dt=F32):
        t = es.enter_context(nc.sbuf_tensor("s_" + name, list(shape), dt))
        return Tl(t, Buf(name))

    with es:
        cmat = sb("cmat", [128, 6, 128])
        ident = cmat.ap[:, 0, :]
        ones = cmat.ap[:, 1, :]
        triu = cmat.ap[:, 2, :]
        e127 = cmat.ap[:, 3, :]
        mnegL = cmat.ap[:, 4, :]
        mnegT = cmat.ap[:, 5, :]
        cw = sb("cw", [128, DEPTH, 24, 4])
        fcw = sb("fcw", [128, DEPTH, NFT, 3])
        fcb = sb("fcb", [128, DEPTH, NFT])
        ong = sb("ong", [128, DEPTH])
        alog = sb("alog", [8, DEPTH])
        dtb = sb("dtb", [8, DEPTH])
        negA = sb("negA", [8, DEPTH])
        epsc = sb("epsc", [128, 2])
        Sst = sb("Sst", [128, H, 128])
        halo = sb("halo", [128, 24, 3])
        fhalo = sb("fhalo", [128, NFT, 2])
        wst = sb("wst", [128, 8, 128])
        sgb = sb("sgb", [128, 8, 128])
        h = sb("h", [128, KC, TG], BF16)
        a16 = es.enter_context(nc.sbuf_tensor("arena16", [128, NFT * TG], BF16))
        A32N = 22700
        a32 = es.enter_context(nc.sbuf_tensor("arena32", [128, A32N], F32))
        WRN = WRN_ELEMS
        wring_t = es.enter_context(nc.sbuf_tensor("s_wring", [128, WRN], BF16))
        wlive = []
        psb = []
        for i in range(8):
            t = es.enter_context(nc.psum_tensor(f"ps{i}", [128, 512], F32))
            psb.append(Tl(t, Buf(f"ps{i}", excl=True)))
        st = {"ps": 0, "w": 0, "a32": 0}
        try:
            print("sbuf bytes remaining", nc.sbuf_bytes_remaining)
        except Exception:
            pass

        RINGS = {"main": [0, 1, 2, 3, 4, 5, 6], "fe": [0, 1], "rec": [2], "cp": [3, 4, 5, 6], "d": [0, 1, 2, 3], "n": [4, 5, 6]}

        def psum(ring="main"):
            k = "ps_" + ring
            i = st.get(k, 0)
            st[k] = i + 1
            rr_ = RINGS[ring]
            return psb[rr_[i % len(rr_)]]

        def a32_reset():
            st["a32"] = 0

        def a32t(name, n):
            o = st["a32"]
            assert o + n <= A32N, (name, o, n)
            st["a32"] = o + n
            return Tl(a32[:, o:o + n], Buf(name))

        def a32t16(name, n):
            m = (n + 1) // 2
            o = st["a32"]
            assert o + m <= A32N, (name, o, m)
            st["a32"] = o + m
            return Tl(a32[:, o:o + m].bitcast(BF16)[:, 0:n], Buf(name))

        ya = Tl(a16[:, 0:H * TG].rearrange("p (c t) -> p c t", c=H), Buf("ya"))
        yb = Tl(a16[:, H * TG:2 * H * TG].rearrange("p (c t) -> p c t", c=H), Buf("yb"))
        merged = Tl(a16[:, 2 * H * TG:2 * H * TG + KC * TG].rearrange("p (c t) -> p c t", c=KC), Buf("merged"))
        hidden = Tl(a16[:, :].rearrange("p (c t) -> p c t", c=NFT), Buf("hidden"))

        def MM(out, lhsT, rhs, start=True, stop=True, R=(), W=()):
            def f(e):
                lh = lhsT() if callable(lhsT) else lhsT
                rh = rhs() if callable(rhs) else rhs
                return e.matmul(out, lhsT=lh, rhs=rh, start=start, stop=stop)
            P.op(PE, f, R, W)

        def TR(out, in_, idn, R=(), W=()):
            P.op(PE, lambda e: e.transpose(out, in_, idn), R, W)

        def AC(out, in_, func, R=(), W=(), bias=None, scale=None, accum=None):
            kw = {}
            if bias is not None:
                kw["bias"] = bias
            if scale is not None:
                kw["scale"] = scale
            if accum is not None:
                kw["accum_out"] = accum
            P.op(ACT, lambda e: e.activation(out=out, in_=in_, func=func, **kw), R, W)

        def TT(out, a, b, op, R=(), W=()):
            P.op(DVE, lambda e: e.tensor_tensor(out=out, in0=a, in1=b, op=op), R, W)

        def TS(out, a, s1, op0, R=(), W=(), s2=None, op1=None):
            if op1 is None:
                P.op(DVE, lambda e: e.tensor_scalar(out=out, in0=a, scalar1=s1, scalar2=None, op0=op0), R, W)
            else:
                P.op(DVE, lambda e: e.tensor_scalar(out=out, in0=a, scalar1=s1, scalar2=s2, op0=op0, op1=op1), R, W)

        def STT(out, a, s, b, op0, op1, R=(), W=()):
            P.op(DVE, lambda e: e.scalar_tensor_tensor(out=out, in0=a, scalar=s, in1=b, op0=op0, op1=op1), R, W)

        def CP(out, in_, R=(), W=()):
            P.op(DVE, lambda e: e.tensor_copy(out=out, in_=in_), R, W)

        def RCP(out, in_, R=(), W=()):
            P.op(DVE, lambda e: e.reciprocal(out=out, in_=in_), R, W)

        def LD(dst, dst_ap, src_ap, R=(), q=SP, weight=False):
            return P.dma(q, lambda e: e.dma_start(out=dst_ap, in_=src_ap), reads=R, writes=[dst.b],
                         sem_buf=dst.b, weight=weight)

        def STO(src, src_ap, dst_ap, W=(), final=False):
            o = P.dma(SP, lambda e: e.dma_start(out=dst_ap, in_=src_ap), reads=[src.b], writes=W,
                      sem_buf=src.b, sem_kind="r")
            if final:
                P.out_dmas.append(o)
            return o

        class WU:
            __slots__ = ("view", "b")

            def s(self, kc, c0, c1):
                return lambda: self.view[:, kc, c0:c1]

        def WLOAD(src3, kc, cols):
            n = kc * cols
            st["wn"] = st.get("wn", 0) + 1
            wn = st["wn"]
            u = WU()
            u.view = None
            u.b = Buf("wr%d" % (wn % 8))

            def do():
                o = st["w"]
                if o + n > WRN:
                    o = 0
                st["w"] = o + n
                over = [t for t in wlive if t[0] < o + n and t[1] > o]
                for t in over:
                    wlive.remove(t)
                wlive.append((o, o + n, u.b))
                u.view = wring_t[:, o:o + n].rearrange("p (k c) -> p k c", k=kc)
                view = u.view
                key = "wprev_%d" % (wn % 8)
                prev = st.get(key)
                op_ = P._dma(POOL, lambda e: e.dma_start(out=view, in_=src3), [],
                             [u.b] + [t[2] for t in over], u.b, "w", True)
                if prev is not None:
                    op_.deps.append(prev)
                st[key] = op_
            if P.rec is not None:
                P.rec.append(do)
            else:
                do()
            return u, u

        def wsrc(w, l, r0, nk, c0, cols):
            return w[l, r0:r0 + nk * 128, c0:c0 + cols].rearrange("(k p) c -> p k c", p=128)

        outB = [[Buf(f"outrow{i}_{c}") for c in range(4)] for i in range(SEQ // 128)]

        LD(cmat, cmat.ap[:], cmat_d)
        LD(cw, cw.ap[:], cw_d)
        LD(fcw, fcw.ap[:], fcw_d)
        LD(fcb, fcb.ap[:], fcb_d)
        LD(ong, ong.ap[:], ong_d)
        LD(alog, alog.ap[:], alog_d)
        LD(dtb, dtb.ap[:], dtb_d)
        AC(negA.ap[:], alog.ap[:], AF.Exp, R=[alog.b], W=[negA.b])
        TS(negA.ap[:], negA.ap[:], -1.0, ALU.mult, R=[negA.b], W=[negA.b])
        P.op(DVE, lambda e: e.memset(epsc.ap[:, 0:1], EPS), (), [epsc.b])
        P.op(DVE, lambda e: e.memset(epsc.ap[:, 1:2], EPS * 128.0), (), [epsc.b])

        def norm_to_h(src_rows, src_bufs, gidx):
            st["a32"] = 10240
            gB = a32t("gB", D)
            xts = [a32t("xt%d" % i_, D) for i_ in range(NB)]
            LD(gB, gB.ap, gn_d[gidx])
            for ti in range(NB):
                LD(xts[ti], xts[ti].ap, src_rows(ti), R=src_bufs(ti))
            P.barrier()
            a32_reset()
            hns = [a32t("hn0", D), a32t("hn1", D)]
            junk = a32t("junk", D)
            sms = [a32t("nsm0", 8), a32t("nsm1", 8)]
            for ti in range(NB):
                xt = xts[ti]
                hn = hns[ti % 2]
                sm = sms[ti % 2]
                ssq = sm.ap[:, 0:1]
                rt = sm.ap[:, 1:2]
                rstd = sm.ap[:, 2:3]
                AC(junk.ap, xt.ap, AF.Square, R=[xt.b], W=[junk.b, sm.b], accum=ssq)
                AC(rt, ssq, AF.Sqrt, R=[sm.b, epsc.b], W=[sm.b], bias=epsc.ap[:, 0:1], scale=1.0 / D)
                RCP(rstd, rt, R=[sm.b], W=[sm.b])
                STT(hn.ap, xt.ap, rstd, gB.ap, ALU.mult, ALU.mult, R=[xt.b, sm.b, gB.b], W=[hn.b])
                for fg in range(KC // 4):
                    ps = psum()
                    for j in range(4):
                        fc = fg * 4 + j
                        TR(ps.ap[:, j * 128:(j + 1) * 128], hn.ap[:, fc * 128:(fc + 1) * 128], ident,
                           R=[hn.b, cmat.b], W=[ps.b])
                    dst = h.ap[:, fg * 4:fg * 4 + 4, ti * 128:(ti + 1) * 128]
                    src = ps.ap[:, :].rearrange("p (a b) -> p a b", a=4)
                    if fg % 2 == 0:
                        AC(dst, src, AF.Copy, R=[ps.b], W=[h.b])
                    else:
                        CP(dst, src, R=[ps.b], W=[h.b])

        def proj_fm(wv, wt, col0, m, ps_ap, psb_, rhs_t=None, nk=KC):
            rt_ = rhs_t if rhs_t is not None else h
            for kc in range(nk):
                MM(ps_ap, wv.s(kc, col0, col0 + m), rt_.ap[:, kc, :], start=(kc == 0), stop=(kc == nk - 1),
                   R=[wt.b, rt_.b], W=[psb_.b])

        dbg_i = [0]

        def DBG(tile, ap):
            if dbg_d is None or dbg_i[0] >= 16:
                return
            i = dbg_i[0]
            dbg_i[0] += 1
            t = sb(f"dbgt{i}", [128, 512])
            P.op(DVE, lambda e: e.memset(t.ap[:], 0.0), (), [t.b])
            n = ap.shape[-1] if len(ap.shape) == 2 else None
            CP(t.ap[0:ap.shape[0], 0:ap.shape[1]], ap, R=[tile.b], W=[t.b])
            STO(t, t.ap[:], dbg_d[i], final=True)

        try:
          _chk(0)
          for l in range(depth):
              P.barrier()
              LD(wst, wst.ap[:], wst_d[l])
              LD(sgb, sgb.ap[:], sgb_d[l])
              for gr in range(8):
                  TT(wst.ap[:, gr, :], wst.ap[:, gr, :], triu, ALU.mult, R=[wst.b, cmat.b], W=[wst.b])
              P.op(DVE, lambda e: e.memset(Sst.ap[:], 0.0), (), [Sst.b])
              P.op(DVE, lambda e: e.memset(halo.ap[:], 0.0), (), [halo.b])
              P.op(DVE, lambda e: e.memset(fhalo.ap[:], 0.0), (), [fhalo.b])
              for g in range(ntg):
                  t0 = g * TG
                  xsrc = x_in if l == 0 else out_d

                  def rows(ti, src=xsrc, t0=t0):
                      return src[t0 + ti * 128:t0 + (ti + 1) * 128, :]

                  def rbufs(ti, l=l, g=g):
                      return [] if l == 0 else list(outB[g * NB + ti])

                  norm_to_h(rows, rbufs, 2 * l)
                  if l == 0 and g == 0:
                      DBG(h, h.ap[:, 0, :])
                  _chk(1)
                  P.barrier()
                  a32_reset()
                  beta_fm = a32t("beta_fm", TG)
                  g_fm = a32t("g_fm", TG)
                  G_fm = a32t("G_fm", TG)
                  cols = a32t("cols", 8 * 32)
                  BETA, GRAW, GCOL, BG, KDC, EGL, NBETA, TMP = [cols.ap[:, i * 32:(i + 1) * 32] for i in range(8)]
                  wv, wt = WLOAD(wsrc(w_in, l, 0, KC, C_B, 16), KC, 16)
                  psB = psum()
                  proj_fm(wv, wt, 0, 8, psB.ap[0:8, :], psB)
                  psA = psum()
                  proj_fm(wv, wt, 8, 8, psA.ap[0:8, :], psA)
                  AC(beta_fm.ap[0:8, :], psB.ap[0:8, :], AF.Sigmoid, R=[psB.b], W=[beta_fm.b])
                  AC(g_fm.ap[0:8, :], psA.ap[0:8, :], AF.Exp, R=[psA.b, dtb.b], W=[g_fm.b], bias=dtb.ap[:, l:l + 1])
                  AC(g_fm.ap[0:8, :], g_fm.ap[0:8, :], AF.Ln, R=[g_fm.b], W=[g_fm.b], bias=1.0)
                  TS(g_fm.ap[0:8, :], g_fm.ap[0:8, :], negA.ap[:, l:l + 1], ALU.mult, R=[g_fm.b, negA.b], W=[g_fm.b])
                  psT = psum()
                  for blk in range(NB):
                      TR(psT.ap[:, blk * 8:(blk + 1) * 8], beta_fm.ap[0:8, blk * 128:(blk + 1) * 128], cmat.ap[0:8, 0, 0:8],
                         R=[beta_fm.b, cmat.b], W=[psT.b])
                      TR(psT.ap[:, 32 + blk * 8:32 + (blk + 1) * 8], g_fm.ap[0:8, blk * 128:(blk + 1) * 128],
                         cmat.ap[0:8, 0, 0:8], R=[g_fm.b, cmat.b], W=[psT.b])
                  CP(cols.ap[:, 0:64], psT.ap[:, 0:64], R=[psT.b], W=[cols.b])
                  psG = psum()
                  MM(psG.ap[:, 0:32], triu, GRAW, R=[cols.b, cmat.b], W=[psG.b])
                  CP(GCOL, psG.ap[:, 0:32], R=[psG.b], W=[cols.b])
                  MM(psG.ap[:, 32:64], e127, GCOL, R=[cols.b, cmat.b], W=[psG.b])
                  psF = psum()
                  for blk in range(NB):
                      MM(psF.ap[0:8, blk * 128:(blk + 1) * 128], cols.ap[:, 32 + blk * 8:32 + (blk + 1) * 8], triu,
                         R=[cols.b, cmat.b], W=[psF.b])
                  CP(G_fm.ap[0:8, :], psF.ap[0:8, :], R=[psF.b], W=[G_fm.b])
                  AC(BG, GCOL, AF.Exp, R=[cols.b], W=[cols.b])
                  TT(BG, BG, BETA, ALU.mult, R=[cols.b], W=[cols.b])
                  TT(TMP, psG.ap[:, 32:64], GCOL, ALU.subtract, R=[psG.b, cols.b], W=[cols.b])
                  AC(KDC, TMP, AF.Exp, R=[cols.b], W=[cols.b])
                  AC(EGL, psG.ap[:, 32:64], AF.Exp, R=[psG.b], W=[cols.b])
                  TS(NBETA, BETA, -1.0, ALU.mult, R=[cols.b], W=[cols.b])

                  if l == 0 and g == 0:
                      DBG(cols, cols.ap[:, 0:256])
                      DBG(G_fm, G_fm.ap[0:8, :])
                  _chk(2)
                  pre = a32t("pre", TG + 3)
                  acc = a32t("acc", TG)
                  sil = a32t("sil", TG)
                  sq = a32t("sq", TG)
                  rr = a32t("rr", TG)
                  pre_k = a32t("pre_k", TG + 3)
                  acc_k = a32t("acc_k", TG)
                  sil_k = a32t("sil_k", TG)
                  sq_k = a32t("sq_k", TG)
                  rr_k = a32t("rr_k", TG)
                  QKVZ = [[a32t16(f"qT{p}", TG), a32t16(f"kT{p}", TG), a32t(f"vT{p}", TG), a32t(f"zs{p}", TG)] for p in range(2)]
                  gam = a32t("gam", TG)
                  gmask = a32t("gmask", TG)
                  qds = [a32t16("qd0", TG), a32t16("qd1", TG)]
                  Sbf = a32t16("Sbf", H * 128)
                  identb = a32t16("identb", 128)
                  oT = a32t("oT", TG)
                  osq = a32t("osq", TG)
                  orr = a32t("orr", TG)
                  BT = []
                  for blk in range(NB):
                      d_ = {}
                      for nm in ("EL", "ET", "Q0", "Q1", "QT0", "QT1", "P0", "P1"):
                          d_[nm] = a32t(f"{nm}_{blk}", 128)
                      for nm in ("kbg", "vbt", "Rb"):
                          d_[nm] = a32t16(f"{nm}_{blk}", 128)
                      BT.append(d_)
                  BTD = [[{"ATb": a32t16(f"ATb_{p}_{blk}", 128), "kdt": a32t16(f"kdt_{p}_{blk}", 128),
                           "ut": a32t(f"ut_{p}_{blk}", 128), "wTt": a32t16(f"wTt_{p}_{blk}", 128)} for blk in range(NB)]
                         for p in range(2)]
                  vnews = [a32t16("vnew0", 128), a32t16("vnew1", 128)]
                  CP(identb.ap, ident, R=[cmat.b], W=[identb.b])
                  AC(Sbf.ap, Sst.ap[:].rearrange("p h d -> p (h d)"), AF.Copy, R=[Sst.b], W=[Sbf.b])
                  BS = [slice(blk * 128, (blk + 1) * 128) for blk in range(NB)]

                  def conv_tile(ct, ps, pre, acc, l=l):
                      AC(pre.ap[:, 3:3 + TG], ps.ap[:, :], AF.Copy, R=[ps.b], W=[pre.b])
                      CP(pre.ap[:, 0:3], halo.ap[:, ct, :], R=[halo.b], W=[pre.b])
                      TS(acc.ap, pre.ap[:, 3:3 + TG], cw.ap[:, l, ct, 3:4], ALU.mult, R=[pre.b, cw.b], W=[acc.b])
                      for j in range(3):
                          STT(acc.ap, pre.ap[:, j:j + TG], cw.ap[:, l, ct, j:j + 1], acc.ap, ALU.mult, ALU.add,
                              R=[pre.b, cw.b, acc.b], W=[acc.b])
                      CP(halo.ap[:, ct, :], pre.ap[:, TG:TG + 3], R=[pre.b], W=[halo.b])

                  def FE(hd, l=l):
                      qT, kT, vT, zs = QKVZ[hd % 2]

                      def proj(which):
                          c0 = (C_QKV + (which * 8 + hd) * 128) if which < 3 else (C_Z + hd * 128)
                          wv, wt = WLOAD(wsrc(w_in, l, 0, KC, c0, 128), KC, 128)
                          ps = psum("fe")
                          proj_fm(wv, wt, 0, 128, ps.ap[:, :], ps)
                          return ps
                      pq = proj(0)
                      pk = proj(1)

                      def chain(which, dst, pp, T5):
                          pre_, acc_, sil_, sq_, rr_ = T5
                          conv_tile(which * 8 + hd, pp, pre_, acc_)
                          AC(sil_.ap, acc_.ap, AF.Silu, R=[acc_.b], W=[sil_.b])
                          AC(sq_.ap, sil_.ap, AF.Square, R=[sil_.b], W=[sq_.b])
                          psn = psum("fe")
                          MM(psn.ap[:, :], ones, sq_.ap, R=[sq_.b, cmat.b], W=[psn.b])
                          if which == 0:
                              AC(rr_.ap, psn.ap[:, :], AF.Sqrt, R=[psn.b, epsc.b], W=[rr_.b], bias=epsc.ap[:, 1:2], scale=128.0)
                          else:
                              AC(rr_.ap, psn.ap[:, :], AF.Sqrt, R=[psn.b, epsc.b], W=[rr_.b], bias=epsc.ap[:, 0:1], scale=1.0)
                          RCP(rr_.ap, rr_.ap, R=[rr_.b], W=[rr_.b])
                          TT(dst.ap, sil_.ap, rr_.ap, ALU.mult, R=[sil_.b, rr_.b], W=[dst.b])
                      outer = P.rec
                      P.rec = []
                      chain(0, qT, pq, (pre, acc, sil, sq, rr))
                      cq = P.rec
                      P.rec = []
                      chain(1, kT, pk, (pre_k, acc_k, sil_k, sq_k, rr_k))
                      ck = P.rec
                      P.rec = outer
                      for i_ in range(max(len(cq), len(ck))):
                          if i_ < len(cq):
                              outer.append(cq[i_])
                          if i_ < len(ck):
                              outer.append(ck[i_])
                      pv = proj(2)
                      pz = proj(3)
                      conv_tile(2 * 8 + hd, pv, pre, acc)
                      AC(vT.ap, acc.ap, AF.Silu, R=[acc.b], W=[vT.b])
                      AC(zs.ap, pz.ap[:, :], AF.Silu, R=[pz.b], W=[zs.b])

                  def CPS(hd):
                      qT, kT, vT, zs = QKVZ[hd % 2]
                      qd = qds[hd % 2]
                      BD = BTD[hd % 2]
                      CI = [blk * 8 + hd for blk in range(NB)]
                      psR = psb[7]
                      TS(gmask.ap[0:8, :], G_fm.ap[0:8, :], cmat.ap[0:8, 0, hd:hd + 1], ALU.mult, R=[G_fm.b, cmat.b], W=[gmask.b])
                      MM(psR.ap[:, :], cmat.ap[0:8, 1, :], gmask.ap[0:8, :], R=[cmat.b, gmask.b], W=[psR.b])
                      AC(gam.ap, psR.ap[:, :], AF.Exp, R=[psR.b], W=[gam.b])
                      TT(qd.ap, qT.ap, gam.ap, ALU.mult, R=[qT.b, gam.b], W=[qd.b])
                      for blk in range(NB):
                          t, d = BT[blk], BD[blk]
                          gcol = GCOL[:, CI[blk]:CI[blk] + 1]
                          STT(t["EL"].ap, psR.ap[:, BS[blk]], gcol, mnegL, ALU.subtract, ALU.subtract,
                              R=[psR.b, cols.b, cmat.b], W=[t["EL"].b])
                          STT(t["ET"].ap, psR.ap[:, BS[blk]], gcol, mnegT, ALU.subtract, ALU.add,
                              R=[psR.b, cols.b, cmat.b], W=[t["ET"].b])
                      for blk in range(NB):
                          t, d = BT[blk], BD[blk]
                          AC(t["EL"].ap, t["EL"].ap, AF.Exp, R=[t["EL"].b], W=[t["EL"].b], scale=-1.0)
                          AC(t["ET"].ap, t["ET"].ap, AF.Exp, R=[t["ET"].b], W=[t["ET"].b])
                      psKs = []
                      for blk in range(NB):
                          bs = BS[blk]
                          psK = psum("cp")
                          MM(psK.ap[:, 0:128], kT.ap[:, bs], kT.ap[:, bs], R=[kT.b], W=[psK.b])
                          MM(psK.ap[:, 128:256], kT.ap[:, bs], qT.ap[:, bs], R=[kT.b, qT.b], W=[psK.b])
                          psKs.append(psK)
                      for blk in range(NB):
                          t, d = BT[blk], BD[blk]
                          psK = psKs[blk]
                          STT(t["EL"].ap, psK.ap[:, 0:128], NBETA[:, CI[blk]:CI[blk] + 1], t["EL"].ap, ALU.mult, ALU.mult,
                              R=[psK.b, cols.b, t["EL"].b], W=[t["EL"].b])
                          TT(d["ATb"].ap, psK.ap[:, 128:256], t["ET"].ap, ALU.mult, R=[psK.b, t["ET"].b], W=[d["ATb"].b])
                      cur = []
                      for blk in range(NB):
                          t = BT[blk]
                          psN = psum("cp")
                          TR(psN.ap[:, 0:128], t["EL"].ap, ident, R=[t["EL"].b, cmat.b], W=[psN.b])
                          AC(t["Q0"].ap, psN.ap[:, 0:128], AF.Copy, R=[psN.b], W=[t["Q0"].b])
                          TT(t["P0"].ap, psN.ap[:, 0:128], ident, ALU.add, R=[psN.b, cmat.b], W=[t["P0"].b])
                          cur.append([t["Q0"], t["EL"], t["P0"]])
                      for m in range(1, 7):
                          banks = []
                          for blk in range(NB):
                              Q, QT, Pm = cur[blk]
                              pb = psum("cp")
                              if m < 6:
                                  MM(pb.ap[:, 0:128], QT.ap, Q.ap, R=[QT.b, Q.b], W=[pb.b])
                              MM(pb.ap[:, 128:256], Q.ap, QT.ap, R=[QT.b, Q.b], W=[pb.b])
                              banks.append(pb)
                          for blk in range(NB):
                              t = BT[blk]
                              pb = banks[blk]
                              Qn = t["Q%d" % (m % 2)]
                              QTn = t["QT%d" % (m % 2)]
                              if m < 6:
                                  AC(Qn.ap, pb.ap[:, 0:128], AF.Copy, R=[pb.b], W=[Qn.b])
                              AC(QTn.ap, pb.ap[:, 128:256], AF.Copy, R=[pb.b], W=[QTn.b])
                          for blk in range(NB):
                              t = BT[blk]
                              pb = banks[blk]
                              Pm = cur[blk][2]
                              QTn = t["QT%d" % (m % 2)]
                              MM(pb.ap[:, 256:384], QTn.ap, Pm.ap, R=[QTn.b, Pm.b], W=[pb.b])
                          for blk in range(NB):
                              t = BT[blk]
                              pb = banks[blk]
                              Pm = cur[blk][2]
                              Pn = t["P%d" % (m % 2)] if m < 6 else t["Rb"]
                              TT(Pn.ap, pb.ap[:, 256:384], Pm.ap, ALU.add, R=[pb.b, Pm.b], W=[Pn.b])
                              cur[blk] = [t["Q%d" % (m % 2)], t["QT%d" % (m % 2)], Pn]
                      psXs = []
                      for blk in range(NB):
                          bs = BS[blk]
                          psX = psum("cp")
                          TR(psX.ap[:, 0:64].bitcast(BF16), kT.ap[:, bs], identb.ap, R=[kT.b, identb.b], W=[psX.b])
                          TR(psX.ap[:, 128:256], vT.ap[:, bs], ident, R=[vT.b, cmat.b], W=[psX.b])
                          psXs.append(psX)
                      for blk in range(NB):
                          t, d = BT[blk], BD[blk]
                          psX = psXs[blk]
                          ci = CI[blk]
                          AC(t["kbg"].ap, psX.ap[:, 0:64].bitcast(BF16), AF.Copy, R=[psX.b, cols.b], W=[t["kbg"].b], scale=BG[:, ci:ci + 1])
                          AC(d["kdt"].ap, psX.ap[:, 0:64].bitcast(BF16), AF.Copy, R=[psX.b, cols.b], W=[d["kdt"].b], scale=KDC[:, ci:ci + 1])
                          AC(t["vbt"].ap, psX.ap[:, 128:256], AF.Copy, R=[psX.b, cols.b], W=[t["vbt"].b], scale=BETA[:, ci:ci + 1])
                      psUs = []
                      for blk in range(NB):
                          t = BT[blk]
                          Rm = t["Rb"]
                          psU = psum("cp")
                          MM(psU.ap[:, 0:128], Rm.ap, t["vbt"].ap, R=[Rm.b, t["vbt"].b], W=[psU.b])
                          MM(psU.ap[:, 128:256], t["kbg"].ap, Rm.ap, R=[Rm.b, t["kbg"].b], W=[psU.b])
                          psUs.append(psU)
                      for blk in range(NB):
                          d = BD[blk]
                          psU = psUs[blk]
                          CP(d["ut"].ap, psU.ap[:, 0:128], R=[psU.b], W=[d["ut"].b])
                          CP(d["wTt"].ap, psU.ap[:, 128:256], R=[psU.b], W=[d["wTt"].b])

                  def REC(hd, l=l):
                      qT, kT, vT, zs = QKVZ[hd % 2]
                      qd = qds[hd % 2]
                      BD = BTD[hd % 2]
                      Sh = Sst.ap[:, hd, :]
                      Shb = Sbf.ap[:, hd * 128:(hd + 1) * 128]
                      for blk in range(NB):
                          d = BD[blk]
                          bs = BS[blk]
                          ci = blk * 8 + hd
                          vnew = vnews[blk % 2]
                          psa = psum("rec")
                          MM(psa.ap[:, 0:128], d["wTt"].ap, Shb, R=[d["wTt"].b, Sbf.b], W=[psa.b])
                          TT(vnew.ap, d["ut"].ap, psa.ap[:, 0:128], ALU.subtract, R=[d["ut"].b, psa.b], W=[vnew.b])
                          MM(psa.ap[:, 128:256], Shb, qd.ap[:, bs], start=True, stop=False, R=[Sbf.b, qd.b], W=[psa.b])
                          MM(psa.ap[:, 128:256], vnew.ap, d["ATb"].ap, start=False, stop=True, R=[vnew.b, d["ATb"].b], W=[psa.b])
                          MM(psa.ap[:, 256:384], d["kdt"].ap, vnew.ap, R=[d["kdt"].b, vnew.b], W=[psa.b])
                          STT(Sh, Sh, EGL[:, ci:ci + 1], psa.ap[:, 256:384], ALU.mult, ALU.add,
                              R=[Sst.b, cols.b, psa.b], W=[Sst.b])
                          AC(Shb, Sh, AF.Copy, R=[Sst.b], W=[Sbf.b])
                          AC(oT.ap[:, bs], psa.ap[:, 128:256], AF.Copy, R=[psa.b], W=[oT.b])
                      AC(osq.ap, oT.ap, AF.Square, R=[oT.b], W=[osq.b])
                      psn = psum("rec")
                      MM(psn.ap[:, :], ones, osq.ap, R=[osq.b, cmat.b], W=[psn.b])
                      AC(orr.ap, psn.ap[:, :], AF.Sqrt, R=[psn.b, epsc.b], W=[orr.b], bias=epsc.ap[:, 0:1], scale=1.0 / 128.0)
                      RCP(orr.ap, orr.ap, R=[orr.b], W=[orr.b])
                      STT(osq.ap, oT.ap, ong.ap[:, l:l + 1], orr.ap, ALU.mult, ALU.mult, R=[oT.b, ong.b, orr.b], W=[osq.b])
                      TT(ya.ap[:, hd, :], osq.ap, zs.ap, ALU.mult, R=[osq.b, zs.b], W=[ya.b])

                  P.begin()
                  FE(0)
                  P.play(P.end())
                  for k in range(H):
                      P.begin()
                      CPS(k)
                      sA = P.end()
                      sB = None
                      if k >= 1:
                          P.begin()
                          REC(k - 1)
                          sB = P.end()
                      sC = None
                      if k < H - 1:
                          P.begin()
                          FE(k + 1)
                          sC = P.end()
                      P.play(sA, sB, sC)
                  P.begin()
                  REC(H - 1)
                  P.play(P.end())

                  if l == 0 and g == 0:
                      DBG(ya, ya.ap[:, 0, :])
                  _chk(4)
                  P.barrier()
                  a32_reset()
                  sil = a32t("sil_g", TG)
                  gvs = [a32t(f"gv{i}", 1024) for i in range(NB)]
                  vg = a32t("vg", 1024)
                  lsm = a32t("lsm", 8)
                  lng = a32t("lng", 1024)
                  lnb = a32t("lnb", 1024)
                  LD(lng, lng.ap, lng_d[l])
                  LD(lnb, lnb.ap, lnb_d[l])
                  for half in range(2):
                      wv, wt = WLOAD(wsrc(w_in, l, 0, KC, C_V + half * 512, 512), KC, 512)
                      for ti in range(NB):
                          tsl = slice(ti * 128, (ti + 1) * 128)
                          ps = psum()
                          for kc in range(KC):
                              MM(ps.ap[:, :], h.ap[:, kc, tsl], wv.s(kc, 0, 512), start=(kc == 0), stop=(kc == KC - 1),
                                 R=[h.b, wt.b], W=[ps.b])
                          AC(gvs[ti].ap[:, half * 512:(half + 1) * 512], ps.ap[:, :], AF.Gelu, R=[ps.b], W=[gvs[ti].b])
                  for ti in range(NB):
                      tsl = slice(ti * 128, (ti + 1) * 128)
                      gv = gvs[ti]
                      s1, s2, mean, msq, var, rstd, nmr = [lsm.ap[:, i:i + 1] for i in range(7)]
                      AC(vg.ap, gv.ap, AF.Identity, R=[gv.b], W=[vg.b, lsm.b], accum=s1)
                      AC(vg.ap, gv.ap, AF.Square, R=[gv.b], W=[vg.b, lsm.b], accum=s2)
                      TS(mean, s1, 1.0 / 1024.0, ALU.mult, R=[lsm.b], W=[lsm.b])
                      TT(msq, mean, mean, ALU.mult, R=[lsm.b], W=[lsm.b])
                      STT(var, s2, 1.0 / 1024.0, msq, ALU.mult, ALU.subtract, R=[lsm.b], W=[lsm.b])
                      AC(var, var, AF.Sqrt, R=[lsm.b, epsc.b], W=[lsm.b], bias=epsc.ap[:, 0:1], scale=1.0)
                      RCP(rstd, var, R=[lsm.b], W=[lsm.b])
                      STT(nmr, mean, -1.0, rstd, ALU.mult, ALU.mult, R=[lsm.b], W=[lsm.b])
                      TS(vg.ap, gv.ap, rstd, ALU.mult, R=[gv.b, lsm.b], W=[vg.b], s2=nmr, op1=ALU.add)
                      TT(vg.ap, vg.ap, lng.ap, ALU.mult, R=[vg.b, lng.b], W=[vg.b])
                      TT(vg.ap, vg.ap, lnb.ap, ALU.add, R=[vg.b, lnb.b], W=[vg.b])
                      for gq in range(2):
                          ps = psum()
                          for j in range(4):
                              gr = gq * 4 + j
                              MM(ps.ap[:, j * 128:(j + 1) * 128], vg.ap[:, gr * 128:(gr + 1) * 128], wst.ap[:, gr, :],
                                 R=[vg.b, wst.b], W=[ps.b])
                          TT(yb.ap[:, gq * 4:gq * 4 + 4, tsl], ps.ap[:, :].rearrange("p (a b) -> p a b", a=4),
                             sgb.ap[:, gq * 4:gq * 4 + 4, :], ALU.add, R=[ps.b, sgb.b], W=[yb.b])
                  for gp in range(4):
                      wv, wt = WLOAD(wsrc(w_in, l, 0, KC, C_U + gp * 256, 256), KC, 256)
                      for j2 in range(2):
                          gr = gp * 2 + j2
                          ps = psum()
                          proj_fm(wv, wt, j2 * 128, 128, ps.ap[:, :], ps)
                          AC(sil.ap, ps.ap[:, :], AF.Gelu, R=[ps.b], W=[sil.b])
                          TT(yb.ap[:, gr, :], yb.ap[:, gr, :], sil.ap, ALU.mult, R=[yb.b, sil.b], W=[yb.b])

                  if l == 0 and g == 0:
                      DBG(yb, yb.ap[:, 0, :])
                  _chk(5)
                  P.barrier()
                  a32_reset()
                  s3 = a32t("s3", TG)
                  s4 = a32t("s4", TG)
                  t1 = a32t("t1", TG)
                  t2 = a32t("t2", TG)
                  for fp in range(KC // 2):
                      wva, wta = WLOAD(wsrc(w_ba, l, 0, 8, fp * 256, 256), 8, 256)
                      wvb, wtb = WLOAD(wsrc(w_bb, l, 0, 8, fp * 256, 256), 8, 256)
                      wvg, wtg = WLOAD(wsrc(w_in, l, 0, KC, C_GA + fp * 256, 256), KC, 256)
                      wvh, wth = WLOAD(wsrc(w_in, l, 0, KC, C_GB + fp * 256, 256), KC, 256)
                      for j2 in range(2):
                          f = fp * 2 + j2
                          c0 = j2 * 128
                          ps1 = psum()
                          proj_fm(wva, wta, c0, 128, ps1.ap[:, :], ps1, rhs_t=ya, nk=8)
                          ps2 = psum()
                          proj_fm(wvb, wtb, c0, 128, ps2.ap[:, :], ps2, rhs_t=yb, nk=8)
                          ps3 = psum()
                          proj_fm(wvg, wtg, c0, 128, ps3.ap[:, :], ps3)
                          ps4 = psum()
                          proj_fm(wvh, wth, c0, 128, ps4.ap[:, :], ps4)
                          AC(s3.ap, ps3.ap[:, :], AF.Sigmoid, R=[ps3.b], W=[s3.b])
                          AC(s4.ap, ps4.ap[:, :], AF.Sigmoid, R=[ps4.b], W=[s4.b])
                          TT(t1.ap, s3.ap, ps1.ap[:, :], ALU.mult, R=[s3.b, ps1.b], W=[t1.b])
                          TT(t2.ap, s4.ap, ps2.ap[:, :], ALU.mult, R=[s4.b, ps2.b], W=[t2.b])
                          TT(merged.ap[:, f, :], t1.ap, t2.ap, ALU.add, R=[t1.b, t2.b], W=[merged.b])
                  xo = [a32t("xo%d" % i_, 512) for i_ in range(8)]
                  xn = [a32t("xq%d" % i_, 512) for i_ in range(16)]
                  gB2 = a32t("gB2", D)
                  hnp = [a32t("hnp0", 512), a32t("hnp1", 512)]
                  njunk = a32t("njunk", 512)
                  ssqp = a32t("ssqp", 16)
                  nsms = [a32t("nsm2_%d" % i_, 8) for i_ in range(NB)]
                  LD(gB2, gB2.ap, gn_d[2 * l + 1])
                  for fo in range(4):
                      for ti in range(NB):
                          xot = xo[(fo % 2) * 4 + ti]
                          ob = outB[g * NB + ti][fo]
                          src = (x_in if l == 0 else out_d)[t0 + ti * 128:t0 + (ti + 1) * 128, fo * 512:(fo + 1) * 512]
                          LD(xot, xot.ap, src, R=([] if l == 0 else [ob]))
                      pso_ = [psum() for _ in range(NB)]
                      for (k0, nk) in ((0, 8), (8, 8)):
                          wv, wt = WLOAD(wsrc(w_out, l, k0 * 128, nk, fo * 512, 512), nk, 512)
                          for ti in range(NB):
                              tsl = slice(ti * 128, (ti + 1) * 128)
                              for kk in range(nk):
                                  f = k0 + kk
                                  MM(pso_[ti].ap[:, :], merged.ap[:, f, tsl], wv.s(kk, 0, 512), start=(f == 0), stop=(f == KC - 1),
                                     R=[merged.b, wt.b], W=[pso_[ti].b])
                      for ti in range(NB):
                          xot, xnt = xo[(fo % 2) * 4 + ti], xn[fo * 4 + ti]
                          ob = outB[g * NB + ti][fo]
                          TT(xnt.ap, xot.ap, pso_[ti].ap[:, :], ALU.add, R=[xot.b, pso_[ti].b], W=[xnt.b])
                          STO(xnt, xnt.ap, out_d[t0 + ti * 128:t0 + (ti + 1) * 128, fo * 512:(fo + 1) * 512], W=[ob])
                          AC(njunk.ap, xnt.ap, AF.Square, R=[xnt.b], W=[njunk.b, ssqp.b],
                             accum=ssqp.ap[:, ti * 4 + fo:ti * 4 + fo + 1])
                  cnt2 = 0
                  for ti in range(NB):
                      sm2 = nsms[ti]
                      c0 = ti * 4
                      ssq, rt, rstd = sm2.ap[:, 0:1], sm2.ap[:, 1:2], sm2.ap[:, 2:3]
                      TT(ssq, ssqp.ap[:, c0:c0 + 1], ssqp.ap[:, c0 + 1:c0 + 2], ALU.add, R=[ssqp.b], W=[sm2.b])
                      TT(ssq, ssq, ssqp.ap[:, c0 + 2:c0 + 3], ALU.add, R=[ssqp.b, sm2.b], W=[sm2.b])
                      TT(ssq, ssq, ssqp.ap[:, c0 + 3:c0 + 4], ALU.add, R=[ssqp.b, sm2.b], W=[sm2.b])
                      AC(rt, ssq, AF.Sqrt, R=[sm2.b, epsc.b], W=[sm2.b], bias=epsc.ap[:, 0:1], scale=1.0 / D)
                      RCP(rstd, rt, R=[sm2.b], W=[sm2.b])
                      for fo in range(4):
                          xnt = xn[fo * 4 + ti]
                          hn2 = hnp[cnt2 % 2]
                          STT(hn2.ap, xnt.ap, rstd, gB2.ap[:, fo * 512:(fo + 1) * 512], ALU.mult, ALU.mult,
                              R=[xnt.b, sm2.b, gB2.b], W=[hn2.b])
                          ps = psum()
                          for j in range(4):
                              TR(ps.ap[:, j * 128:(j + 1) * 128], hn2.ap[:, j * 128:(j + 1) * 128], ident,
                                 R=[hn2.b, cmat.b], W=[ps.b])
                          dst = h.ap[:, fo * 4:fo * 4 + 4, ti * 128:(ti + 1) * 128]
                          srcp = ps.ap[:, :].rearrange("p (a b) -> p a b", a=4)
                          if cnt2 % 2 == 0:
                              AC(dst, srcp, AF.Copy, R=[ps.b], W=[h.b])
                          else:
                              CP(dst, srcp, R=[ps.b], W=[h.b])
                          cnt2 += 1

                  if l == 0 and g == 0:
                      DBG(merged, merged.ap[:, 0, :])
                  _chk(6)
                  P.barrier()
                  a32_reset()
                  pre2 = a32t("pre2", TG + 2)
                  acc2 = a32t("acc2", TG)
                  sg2 = a32t("sg2", TG)
                  for ft in range(NFT):
                      if ft % 2 == 0:
                          wvg, wtg = WLOAD(wsrc(w_gate, l, 0, KC, ft * 128, 256), KC, 256)
                          wvu, wtu = WLOAD(wsrc(w_up, l, 0, KC, ft * 128, 256), KC, 256)
                      c0 = (ft % 2) * 128
                      psg = psum()
                      proj_fm(wvg, wtg, c0, 128, psg.ap[:, :], psg)
                      psu = psum()
                      proj_fm(wvu, wtu, c0, 128, psu.ap[:, :], psu)
                      AC(pre2.ap[:, 2:2 + TG], psg.ap[:, :], AF.Copy, R=[psg.b], W=[pre2.b])
                      CP(pre2.ap[:, 0:2], fhalo.ap[:, ft, :], R=[fhalo.b], W=[pre2.b])
                      TS(acc2.ap, pre2.ap[:, 2:2 + TG], fcw.ap[:, l, ft, 2:3], ALU.mult, R=[pre2.b, fcw.b], W=[acc2.b])
                      for j in range(2):
                          STT(acc2.ap, pre2.ap[:, j:j + TG], fcw.ap[:, l, ft, j:j + 1], acc2.ap, ALU.mult, ALU.add,
                              R=[pre2.b, fcw.b, acc2.b], W=[acc2.b])
                      CP(fhalo.ap[:, ft, :], pre2.ap[:, TG:TG + 2], R=[pre2.b], W=[fhalo.b])
                      AC(sg2.ap, acc2.ap, AF.Silu, R=[acc2.b, fcb.b], W=[sg2.b], bias=fcb.ap[:, l, ft:ft + 1])
                      TT(hidden.ap[:, ft, :], sg2.ap, psu.ap[:, :], ALU.mult, R=[sg2.b, psu.b], W=[hidden.b])
                  xo = [a32t("xo%d" % i_, 512) for i_ in range(8)]
                  xn = [a32t("xn%d" % i_, 512) for i_ in range(8)]
                  parts = [(0, 8), (8, 8), (16, 8), (24, 8), (32, 8), (40, 4)]
                  for fo in range(4):
                      for ti in range(NB):
                          xot = xo[(fo % 2) * 4 + ti]
                          ob = outB[g * NB + ti][fo]
                          dsl = out_d[t0 + ti * 128:t0 + (ti + 1) * 128, fo * 512:(fo + 1) * 512]
                          LD(xot, xot.ap, dsl, R=[ob])
                      pss_ = [psum() for _ in range(NB)]
                      for (k0, nk) in parts:
                          wv, wt = WLOAD(wsrc(w_down, l, k0 * 128, nk, fo * 512, 512), nk, 512)
                          for ti in range(NB):
                              tsl = slice(ti * 128, (ti + 1) * 128)
                              for kk in range(nk):
                                  kc = k0 + kk
                                  MM(pss_[ti].ap[:, :], hidden.ap[:, kc, tsl], wv.s(kk, 0, 512), start=(kc == 0), stop=(kc == NFT - 1),
                                     R=[hidden.b, wt.b], W=[pss_[ti].b])
                      for ti in range(NB):
                          xot, xnt = xo[(fo % 2) * 4 + ti], xn[(fo % 2) * 4 + ti]
                          ob = outB[g * NB + ti][fo]
                          dsl = out_d[t0 + ti * 128:t0 + (ti + 1) * 128, fo * 512:(fo + 1) * 512]
                          TT(xnt.ap, xot.ap, pss_[ti].ap[:, :], ALU.add, R=[xot.b, pss_[ti].b], W=[xnt.b])
                          STO(xnt, xnt.ap, dsl, W=[ob])
        except _Stop:
            pass
        P.barrier()
        a32_reset()
        gB = a32t("gBf", D)
        LD(gB, gB.ap, gn_d[2 * DEPTH])
        NBUF = 4
        xts = [a32t("fx%d" % i_, D) for i_ in range(NBUF)]
        hns = [a32t("fh%d" % i_, D) for i_ in range(NBUF)]
        sms = [a32t("fsm%d" % i_, 8) for i_ in range(NBUF)]
        ntile = ntg * NB

        def fload(ti):
            xt = xts[ti % NBUF]
            LD(xt, xt.ap, out_d[ti * 128:(ti + 1) * 128, :], R=list(outB[ti]))
        for ti in range(min(NBUF - 1, ntile)):
            fload(ti)
        for ti in range(ntile):
            if ti + NBUF - 1 < ntile:
                fload(ti + NBUF - 1)
            xt, hn, sm = xts[ti % NBUF], hns[ti % NBUF], sms[ti % NBUF]
            ssq, rt, rstd = sm.ap[:, 0:1], sm.ap[:, 1:2], sm.ap[:, 2:3]
            AC(hn.ap, xt.ap, AF.Square, R=[xt.b], W=[hn.b, sm.b], accum=ssq)
            AC(rt, ssq, AF.Sqrt, R=[sm.b, epsc.b], W=[sm.b], bias=epsc.ap[:, 0:1], scale=1.0 / D)
            RCP(rstd, rt, R=[sm.b], W=[sm.b])
            STT(hn.ap, xt.ap, rstd, gB.ap, ALU.mult, ALU.mult, R=[xt.b, sm.b, gB.b], W=[hn.b])
            STO(hn, hn.ap, out_d[ti * 128:(ti + 1) * 128, :], W=list(outB[ti]), final=True)
        P.emit()
    return nc


def host_consts():
    i = np.arange(128)
    ident = np.eye(128, dtype=np.float32)
    ones = np.ones((128, 128), np.float32)
    triu = (i[:, None] <= i[None, :]).astype(np.float32)
    e127 = np.zeros((128, 128), np.float32)
    e127[127, :] = 1.0
    NEG = np.float32(-1e30)
    mnegL = np.where(i[:, None] > i[None, :], 0.0, NEG).astype(np.float32)
    mnegT = np.where(i[None, :] >= i[:, None], 0.0, NEG).astype(np.float32)
    cmat = np.stack([ident, ones, triu, e127, mnegL, mnegT], axis=1)
    sel = np.zeros((8, 8, 128), np.float32)
    for hh in range(8):
        sel[hh, hh, :] = 1.0
    return np.ascontiguousarray(cmat), sel


def prep_shared(inp):
    f = lambda a: np.ascontiguousarray(np.asarray(a, dtype=np.float32))
    cmat, sel = host_consts()
    rep = lambda v: np.broadcast_to(np.asarray(v, np.float32)[None, :], (128, v.shape[-1]))
    gn = np.stack([rep(inp["norm1_g"][0]), rep(inp["norm2_g"][0]), rep(inp["norm1_g"][1]), rep(inp["norm2_g"][1]),
                   rep(inp["final_norm_g"])], axis=0)
    cw = np.asarray(inp["dn_conv_w"], np.float32).reshape(DEPTH, 4, 24, 128).transpose(3, 0, 2, 1)
    fcw = np.asarray(inp["ffn_conv_w"], np.float32).reshape(DEPTH, 3, NFT, 128).transpose(3, 0, 2, 1)
    fcb = np.asarray(inp["ffn_conv_b"], np.float32).reshape(DEPTH, NFT, 128).transpose(2, 0, 1)
    ong = np.asarray(inp["dn_onorm_g"], np.float32).T
    alog = np.asarray(inp["dn_a_log"], np.float32).T
    dtb = np.asarray(inp["dn_dt_bias"], np.float32).T
    lng = np.stack([rep(inp["sg_ln_g"][l]) for l in range(DEPTH)], 0)
    lnb = np.stack([rep(inp["sg_ln_b"][l]) for l in range(DEPTH)], 0)
    wst = np.asarray(inp["sg_w"], np.float32).transpose(0, 3, 1, 2)
    sgb = np.broadcast_to(np.asarray(inp["sg_b"], np.float32)[:, None, :, :], (DEPTH, 128, 8, 128))
    shared = {
        "w_in": f(inp["w_in"]), "w_branch_a": f(inp["w_branch_a"]), "w_branch_b": f(inp["w_branch_b"]),
        "w_out": f(inp["w_out"]), "ffn_w_gate": f(inp["ffn_w_gate"]), "ffn_w_up": f(inp["ffn_w_up"]),
        "ffn_w_down": f(inp["ffn_w_down"]), "cmat": cmat, "sel": sel, "gnorm": f(gn), "cw": f(cw), "fcw": f(fcw),
        "fcb": f(fcb), "ong": f(ong), "alog": f(alog), "dtb": f(dtb), "lng": f(lng), "lnb": f(lnb),
        "wst": f(wst), "sgb": f(sgb),
    }
    return shared


def kernel(**inputs):
    x = np.asarray(inputs["x"], dtype=np.float32)
    shared = prep_shared(inputs)
    nc = build_program()
    in_maps = []
    for b in range(BATCH):
        m = dict(shared)
        m["x"] = np.ascontiguousarray(x[b])
        in_maps.append(m)
    res = run_bass_kernel_spmd(nc, in_maps, core_ids=list(range(BATCH)))
    out = np.stack([np.asarray(res.results[b]["out"], dtype=np.float32) for b in range(BATCH)], axis=0)
    return out
```

```python
import contextlib
import numpy as np
import concourse.bass as bass
import concourse.mybir as mybir
from concourse.bass_utils import run_bass_kernel_spmd

F32 = mybir.dt.float32
BF16 = mybir.dt.bfloat16
AF = mybir.ActivationFunctionType
ALU = mybir.AluOpType

PE, ACT, DVE, POOL, SP = "tensor", "scalar", "vector", "gpsimd", "sync"
QUEUES = (PE, ACT, DVE, POOL, SP)

D = 2048
SEQ = 4096
BATCH = 4
DEPTH = 2
H = 8
DFF = 5632
INC = 10256
TG = 512
NB = TG // 128
KC = D // 128
NFT = DFF // 128
EPS = 1e-6
C_QKV, C_Z, C_B, C_A, C_U, C_V, C_GA, C_GB = 0, 3072, 4096, 4104, 4112, 5136, 6160, 8208
WSLOT = 16 * 512
NSLOT = 3
WRN_ELEMS = 20480


class Buf:
    __slots__ = ("name", "last_w", "rd_q", "rd_dma", "wsem", "rsem", "wcnt", "rcnt", "excl")

    def __init__(self, name, excl=False):
        self.name = name
        self.excl = excl
        self.last_w = None
        self.rd_q = {}
        self.rd_dma = []
        self.wsem = None
        self.rsem = None
        self.wcnt = 0
        self.rcnt = 0


class Op:
    __slots__ = ("q", "fn", "deps", "signal", "sigval", "is_dma", "dsem", "dval")

    def __init__(self, q, fn, is_dma=False):
        self.q = q
        self.fn = fn
        self.deps = []
        self.signal = False
        self.sigval = 0
        self.is_dma = is_dma
        self.dsem = None
        self.dval = 0


class Prog:
    def __init__(self, nc):
        self.nc = nc
        self.qops = {q: [] for q in QUEUES}
        self.ctx = contextlib.ExitStack()
        self.bar = {}
        self.out_dmas = []
        self.live_dmas = []
        self.semtab = {}
        self.rec = None

    def new_sem(self, name):
        return self.ctx.enter_context(self.nc.semaphore(name))

    def _track(self, op, reads, writes):
        deps = op.deps
        q = op.q
        comp = not op.is_dma

        def same(d):
            return comp and (not d.is_dma) and d.q == q
        bd = self.bar.get(q)
        if bd:
            for d in bd:
                if not same(d):
                    deps.append(d)
            self.bar[q] = None
        for b in reads:
            lw = b.last_w
            if lw is not None and not (q == PE and same(lw)):
                deps.append(lw)
            if b.excl:
                for r in b.rd_q.values():
                    if not same(r):
                        deps.append(r)
        for b in writes:
            lw = b.last_w
            if lw is not None and not same(lw):
                deps.append(lw)
            for r in b.rd_q.values():
                if not same(r):
                    deps.append(r)
            deps.extend(b.rd_dma)
        for b in reads:
            if op.is_dma:
                b.rd_dma.append(op)
            else:
                b.rd_q[q] = op
        for b in writes:
            b.last_w = op
            b.rd_q = {}
            b.rd_dma = []

    def begin(self):
        assert self.rec is None
        self.rec = []

    def end(self):
        r = self.rec
        self.rec = None
        return r

    def play(self, *streams):
        streams = [x for x in streams if x]
        idx = [0] * len(streams)
        total = sum(len(x) for x in streams)
        for _ in range(total):
            best = min((i for i in range(len(streams)) if idx[i] < len(streams[i])),
                       key=lambda i: (idx[i] + 0.5) / len(streams[i]))
            streams[best][idx[best]]()
            idx[best] += 1

    def op(self, q, fn, reads=(), writes=()):
        if self.rec is not None:
            self.rec.append(lambda: self._op(q, fn, reads, writes))
            return None
        return self._op(q, fn, reads, writes)

    def _op(self, q, fn, reads=(), writes=()):
        o = Op(q, fn)
        self._track(o, reads, writes)
        self.qops[q].append(o)
        return o

    def dma(self, q, fn, reads=(), writes=(), sem_buf=None, sem_kind="w", weight=False):
        if self.rec is not None:
            self.rec.append(lambda: self._dma(q, fn, reads, writes, sem_buf, sem_kind, weight))
            return None
        return self._dma(q, fn, reads, writes, sem_buf, sem_kind, weight)

    def _dma(self, q, fn, reads=(), writes=(), sem_buf=None, sem_kind="w", weight=False):
        o = Op(q, fn, is_dma=True)
        self._track(o, reads, writes)
        key = ("w_" if sem_kind == "w" else "r_") + sem_buf.name
        ent = self.semtab.get(key)
        if ent is None:
            ent = [self.new_sem(key), 0]
            self.semtab[key] = ent
        ent[1] += 16
        o.dsem, o.dval = ent[0], ent[1]
        self.qops[q].append(o)
        if not weight:
            self.live_dmas.append(o)
        return o

    def barrier(self):
        deps = []
        for q in (PE, ACT, DVE):
            for o in reversed(self.qops[q]):
                if not o.is_dma:
                    deps.append(o)
                    break
        deps.extend(self.live_dmas)
        self.live_dmas = []
        for q in (PE, ACT, DVE, SP):
            self.bar[q] = list(deps)

    def emit(self):
        nc = self.nc
        for q in QUEUES:
            for o in self.qops[q]:
                for d in o.deps:
                    if not d.is_dma:
                        d.signal = True
        qsem = {}
        for q in (PE, ACT, DVE):
            cnt = 0
            for o in self.qops[q]:
                if o.signal:
                    cnt += 1
                    o.sigval = cnt
            qsem[q] = self.new_sem("q_" + q)
        finals = {}
        for o in self.out_dmas:
            k = id(o.dsem)
            if k not in finals or finals[k][1] < o.dval:
                finals[k] = (o.dsem, o.dval)
        with nc.Block() as block:
            def run(q):
                def body(eng):
                    waited = {}
                    for o in self.qops[q]:
                        need = {}
                        for d in o.deps:
                            if d.is_dma:
                                s, v = d.dsem, d.dval
                            else:
                                s, v = qsem[d.q], d.sigval
                            k = id(s)
                            if waited.get(k, 0) >= v:
                                continue
                            if k not in need or need[k][1] < v:
                                need[k] = (s, v)
                        for k, (s, v) in need.items():
                            eng.wait_ge(s, v)
                            waited[k] = v
                        ins = o.fn(eng)
                        if o.is_dma:
                            ins.then_inc(o.dsem, 16)
                        elif o.signal:
                            ins.then_inc(qsem[q], 1)
                    if q == SP:
                        for (s, v) in finals.values():
                            eng.wait_ge(s, v)
                return body
            block.tensor(run(PE))
            block.scalar(run(ACT))
            block.vector(run(DVE))
            block.gpsimd(run(POOL))
            block.sync(run(SP))


class Tl:
    __slots__ = ("ap", "b")

    def __init__(self, ap, b):
        self.ap = ap
        self.b = b


class _Stop(Exception):
    pass


STOP = [99]


def _chk(k):
    if STOP[0] <= k:
        raise _Stop()


def build_program(ntg=SEQ // TG, depth=DEPTH, dbg=False):
    nc = bass.Bass("TRN2", target_bir_lowering=False)
    P = Prog(nc)

    def din(name, shape):
        return nc.dram_tensor(name, list(shape), F32, kind="ExternalInput").ap()

    x_in = din("x", [SEQ, D])
    w_in = din("w_in", [DEPTH, D, INC])
    w_ba = din("w_branch_a", [DEPTH, 1024, D])
    w_bb = din("w_branch_b", [DEPTH, 1024, D])
    w_out = din("w_out", [DEPTH, D, D])
    w_gate = din("ffn_w_gate", [DEPTH, D, DFF])
    w_up = din("ffn_w_up", [DEPTH, D, DFF])
    w_down = din("ffn_w_down", [DEPTH, DFF, D])
    cmat_d = din("cmat", [128, 6, 128])
    sel_d = din("sel", [8, 8, 128])
    gn_d = din("gnorm", [2 * DEPTH + 1, 128, D])
    cw_d = din("cw", [128, DEPTH, 24, 4])
    fcw_d = din("fcw", [128, DEPTH, NFT, 3])
    fcb_d = din("fcb", [128, DEPTH, NFT])
    ong_d = din("ong", [128, DEPTH])
    alog_d = din("alog", [8, DEPTH])
    dtb_d = din("dtb", [8, DEPTH])
    lng_d = din("lng", [DEPTH, 128, 1024])
    lnb_d = din("lnb", [DEPTH, 128, 1024])
    wst_d = din("wst", [DEPTH, 128, 8, 128])
    sgb_d = din("sgb", [DEPTH, 128, 8, 128])
    out_d = nc.dram_tensor("out", [SEQ, D], F32, kind="ExternalOutput").ap()
    dbg_d = None
    if dbg:
        dbg_d = nc.dram_tensor("dbg", [16, 128, 512], F32, kind="ExternalOutput").ap()

    es = P.ctx

    def sb(name, shape, dt=F32):
        t = es.enter_context(nc.sbuf_tensor("s_" + name, list(shape), dt))
        return Tl(t, Buf(name))

    with es:
        cmat = sb("cmat", [128, 6, 128])
        ident = cmat.ap[:, 0, :]
        ones = cmat.ap[:, 1, :]
        triu = cmat.ap[:, 2, :]
        e127 = cmat.ap[:, 3, :]
        mnegL = cmat.ap[:, 4, :]
        mnegT = cmat.ap[:, 5, :]
        cw = sb("cw", [128, DEPTH, 24, 4])
        fcw = sb("fcw", [128, DEPTH, NFT, 3])
        fcb = sb("fcb", [128, DEPTH, NFT])
        ong = sb("ong", [128, DEPTH])
        alog = sb("alog", [8, DEPTH])
        dtb = sb("dtb", [8, DEPTH])
        negA = sb("negA", [8, DEPTH])
        epsc = sb("epsc", [128, 2])
        Sst = sb("Sst", [128, H, 128])
        halo = sb("halo", [128, 24, 3])
        fhalo = sb("fhalo", [128, NFT, 2])
        wst = sb("wst", [128, 8, 128])
        sgb = sb("sgb", [128, 8, 128])
        h = sb("h", [128, KC, TG], BF16)
        a16 = es.enter_context(nc.sbuf_tensor("arena16", [128, NFT * TG], BF16))
        A32N = 22700
        a32 = es.enter_context(nc.sbuf_tensor("arena32", [128, A32N], F32))
        WRN = WRN_ELEMS
        wring_t = es.enter_context(nc.sbuf_tensor("s_wring", [128, WRN], BF16))
        wlive = []
        psb = []
        for i in range(8):
            t = es.enter_context(nc.psum_tensor(f"ps{i}", [128, 512], F32))
            psb.append(Tl(t, Buf(f"ps{i}", excl=True)))
        st = {"ps": 0, "w": 0, "a32": 0}
        try:
            print("sbuf bytes remaining", nc.sbuf_bytes_remaining)
        except Exception:
            pass

        RINGS = {"main": [0, 1, 2, 3, 4, 5, 6], "fe": [0, 1], "rec": [2], "cp": [3, 4, 5, 6], "d": [0, 1, 2, 3], "n": [4, 5, 6]}

        def psum(ring="main"):
            k = "ps_" + ring
            i = st.get(k, 0)
            st[k] = i + 1
            rr_ = RINGS[ring]
            return psb[rr_[i % len(rr_)]]

        def a32_reset():
            st["a32"] = 0

        def a32t(name, n):
            o = st["a32"]
            assert o + n <= A32N, (name, o, n)
            st["a32"] = o + n
            return Tl(a32[:, o:o + n], Buf(name))

        def a32t16(name, n):
            m = (n + 1) // 2
            o = st["a32"]
            assert o + m <= A32N, (name, o, m)
            st["a32"] = o + m
            return Tl(a32[:, o:o + m].bitcast(BF16)[:, 0:n], Buf(name))

        ya = Tl(a16[:, 0:H * TG].rearrange("p (c t) -> p c t", c=H), Buf("ya"))
        yb = Tl(a16[:, H * TG:2 * H * TG].rearrange("p (c t) -> p c t", c=H), Buf("yb"))
        merged = Tl(a16[:, 2 * H * TG:2 * H * TG + KC * TG].rearrange("p (c t) -> p c t", c=KC), Buf("merged"))
        hidden = Tl(a16[:, :].rearrange("p (c t) -> p c t", c=NFT), Buf("hidden"))

        def MM(out, lhsT, rhs, start=True, stop=True, R=(), W=()):
            def f(e):
                lh = lhsT() if callable(lhsT) else lhsT
                rh = rhs() if callable(rhs) else rhs
                return e.matmul(out, lhsT=lh, rhs=rh, start=start, stop=stop)
            P.op(PE, f, R, W)

        def TR(out, in_, idn, R=(), W=()):
            P.op(PE, lambda e: e.transpose(out, in_, idn), R, W)

        def AC(out, in_, func, R=(), W=(), bias=None, scale=None, accum=None):
            kw = {}
            if bias is not None:
                kw["bias"] = bias
            if scale is not None:
                kw["scale"] = scale
            if accum is not None:
                kw["accum_out"] = accum
            P.op(ACT, lambda e: e.activation(out=out, in_=in_, func=func, **kw), R, W)

        def TT(out, a, b, op, R=(), W=()):
            P.op(DVE, lambda e: e.tensor_tensor(out=out, in0=a, in1=b, op=op), R, W)

        def TS(out, a, s1, op0, R=(), W=(), s2=None, op1=None):
            if op1 is None:
                P.op(DVE, lambda e: e.tensor_scalar(out=out, in0=a, scalar1=s1, scalar2=None, op0=op0), R, W)
            else:
                P.op(DVE, lambda e: e.tensor_scalar(out=out, in0=a, scalar1=s1, scalar2=s2, op0=op0, op1=op1), R, W)

        def STT(out, a, s, b, op0, op1, R=(), W=()):
            P.op(DVE, lambda e: e.scalar_tensor_tensor(out=out, in0=a, scalar=s, in1=b, op0=op0, op1=op1), R, W)

        def CP(out, in_, R=(), W=()):
            P.op(DVE, lambda e: e.tensor_copy(out=out, in_=in_), R, W)

        def RCP(out, in_, R=(), W=()):
            P.op(DVE, lambda e: e.reciprocal(out=out, in_=in_), R, W)

        def LD(dst, dst_ap, src_ap, R=(), q=SP, weight=False):
            return P.dma(q, lambda e: e.dma_start(out=dst_ap, in_=src_ap), reads=R, writes=[dst.b],
                         sem_buf=dst.b, weight=weight)

        def STO(src, src_ap, dst_ap, W=(), final=False):
            o = P.dma(SP, lambda e: e.dma_start(out=dst_ap, in_=src_ap), reads=[src.b], writes=W,
                      sem_buf=src.b, sem_kind="r")
            if final:
                P.out_dmas.append(o)
            return o

        class WU:
            __slots__ = ("view", "b")

            def s(self, kc, c0, c1):
                return lambda: self.view[:, kc, c0:c1]

        def WLOAD(src3, kc, cols):
            n = kc * cols
            st["wn"] = st.get("wn", 0) + 1
            wn = st["wn"]
            u = WU()
            u.view = None
            u.b = Buf("wr%d" % (wn % 8))

            def do():
                o = st["w"]
                if o + n > WRN:
                    o = 0
                st["w"] = o + n
                over = [t for t in wlive if t[0] < o + n and t[1] > o]
                for t in over:
                    wlive.remove(t)
                wlive.append((o, o + n, u.b))
                u.view = wring_t[:, o:o + n].rearrange("p (k c) -> p k c", k=kc)
                view = u.view
                key = "wprev_%d" % (wn % 8)
                prev = st.get(key)
                op_ = P._dma(POOL, lambda e: e.dma_start(out=view, in_=src3), [],
                             [u.b] + [t[2] for t in over], u.b, "w", True)
                if prev is not None:
                    op_.deps.append(prev)
                st[key] = op_
            if P.rec is not None:
                P.rec.append(do)
            else:
                do()
            return u, u

        def wsrc(w, l, r0, nk, c0, cols):
            return w[l, r0:r0 + nk * 128, c0:c0 + cols].rearrange("(k p) c -> p k c", p=128)

        outB = [[Buf(f"outrow{i}_{c}") for c in range(4)] for i in range(SEQ // 128)]

        LD(cmat, cmat.ap[:], cmat_d)
        LD(cw, cw.ap[:], cw_d)
        LD(fcw, fcw.ap[:], fcw_d)
        LD(fcb, fcb.ap[:], fcb_d)
        LD(ong, ong.ap[:], ong_d)
        LD(alog, alog.ap[:], alog_d)
        LD(dtb, dtb.ap[:], dtb_d)
        AC(negA.ap[:], alog.ap[:], AF.Exp, R=[alog.b], W=[negA.b])
        TS(negA.ap[:], negA.ap[:], -1.0, ALU.mult, R=[negA.b], W=[negA.b])
        P.op(DVE, lambda e: e.memset(epsc.ap[:, 0:1], EPS), (), [epsc.b])
        P.op(DVE, lambda e: e.memset(epsc.ap[:, 1:2], EPS * 128.0), (), [epsc.b])

        def norm_to_h(src_rows, src_bufs, gidx):
            st["a32"] = 10240
            gB = a32t("gB", D)
            xts = [a32t("xt%d" % i_, D) for i_ in range(NB)]
            LD(gB, gB.ap, gn_d[gidx])
            for ti in range(NB):
                LD(xts[ti], xts[ti].ap, src_rows(ti), R=src_bufs(ti))
            P.barrier()
            a32_reset()
            hns = [a32t("hn0", D), a32t("hn1", D)]
            junk = a32t("junk", D)
            sms = [a32t("nsm0", 8), a32t("nsm1", 8)]
            for ti in range(NB):
                xt = xts[ti]
                hn = hns[ti % 2]
                sm = sms[ti % 2]
                ssq = sm.ap[:, 0:1]
                rt = sm.ap[:, 1:2]
                rstd = sm.ap[:, 2:3]
                AC(junk.ap, xt.ap, AF.Square, R=[xt.b], W=[junk.b, sm.b], accum=ssq)
                AC(rt, ssq, AF.Sqrt, R=[sm.b, epsc.b], W=[sm.b], bias=epsc.ap[:, 0:1], scale=1.0 / D)
                RCP(rstd, rt, R=[sm.b], W=[sm.b])
                STT(hn.ap, xt.ap, rstd, gB.ap, ALU.mult, ALU.mult, R=[xt.b, sm.b, gB.b], W=[hn.b])
                for fg in range(KC // 4):
                    ps = psum()
                    for j in range(4):
                        fc = fg * 4 + j
                        TR(ps.ap[:, j * 128:(j + 1) * 128], hn.ap[:, fc * 128:(fc + 1) * 128], ident,
                           R=[hn.b, cmat.b], W=[ps.b])
                    dst = h.ap[:, fg * 4:fg * 4 + 4, ti * 128:(ti + 1) * 128]
                    src = ps.ap[:, :].rearrange("p (a b) -> p a b", a=4)
                    if fg % 2 == 0:
                        AC(dst, src, AF.Copy, R=[ps.b], W=[h.b])
                    else:
                        CP(dst, src, R=[ps.b], W=[h.b])

        def proj_fm(wv, wt, col0, m, ps_ap, psb_, rhs_t=None, nk=KC):
            rt_ = rhs_t if rhs_t is not None else h
            for kc in range(nk):
                MM(ps_ap, wv.s(kc, col0, col0 + m), rt_.ap[:, kc, :], start=(kc == 0), stop=(kc == nk - 1),
                   R=[wt.b, rt_.b], W=[psb_.b])

        dbg_i = [0]

        def DBG(tile, ap):
            if dbg_d is None or dbg_i[0] >= 16:
                return
            i = dbg_i[0]
            dbg_i[0] += 1
            t = sb(f"dbgt{i}", [128, 512])
            P.op(DVE, lambda e: e.memset(t.ap[:], 0.0), (), [t.b])
            n = ap.shape[-1] if len(ap.shape) == 2 else None
            CP(t.ap[0:ap.shape[0], 0:ap.shape[1]], ap, R=[tile.b], W=[t.b])
            STO(t, t.ap[:], dbg_d[i], final=True)

        try:
          _chk(0)
          for l in range(depth):
              P.barrier()
              LD(wst, wst.ap[:], wst_d[l])
              LD(sgb, sgb.ap[:], sgb_d[l])
              for gr in range(8):
                  TT(wst.ap[:, gr, :], wst.ap[:, gr, :], triu, ALU.mult, R=[wst.b, cmat.b], W=[wst.b])
              P.op(DVE, lambda e: e.memset(Sst.ap[:], 0.0), (), [Sst.b])
              P.op(DVE, lambda e: e.memset(halo.ap[:], 0.0), (), [halo.b])
              P.op(DVE, lambda e: e.memset(fhalo.ap[:], 0.0), (), [fhalo.b])
              for g in range(ntg):
                  t0 = g * TG
                  xsrc = x_in if l == 0 else out_d

                  def rows(ti, src=xsrc, t0=t0):
                      return src[t0 + ti * 128:t0 + (ti + 1) * 128, :]

                  def rbufs(ti, l=l, g=g):
                      return [] if l == 0 else list(outB[g * NB + ti])

                  norm_to_h(rows, rbufs, 2 * l)
                  if l == 0 and g == 0:
                      DBG(h, h.ap[:, 0, :])
                  _chk(1)
                  P.barrier()
                  a32_reset()
                  beta_fm = a32t("beta_fm", TG)
                  g_fm = a32t("g_fm", TG)
                  G_fm = a32t("G_fm", TG)
                  cols = a32t("cols", 8 * 32)
                  BETA, GRAW, GCOL, BG, KDC, EGL, NBETA, TMP = [cols.ap[:, i * 32:(i + 1) * 32] for i in range(8)]
                  wv, wt = WLOAD(wsrc(w_in, l, 0, KC, C_B, 16), KC, 16)
                  psB = psum()
                  proj_fm(wv, wt, 0, 8, psB.ap[0:8, :], psB)
                  psA = psum()
                  proj_fm(wv, wt, 8, 8, psA.ap[0:8, :], psA)
                  AC(beta_fm.ap[0:8, :], psB.ap[0:8, :], AF.Sigmoid, R=[psB.b], W=[beta_fm.b])
                  AC(g_fm.ap[0:8, :], psA.ap[0:8, :], AF.Exp, R=[psA.b, dtb.b], W=[g_fm.b], bias=dtb.ap[:, l:l + 1])
                  AC(g_fm.ap[0:8, :], g_fm.ap[0:8, :], AF.Ln, R=[g_fm.b], W=[g_fm.b], bias=1.0)
                  TS(g_fm.ap[0:8, :], g_fm.ap[0:8, :], negA.ap[:, l:l + 1], ALU.mult, R=[g_fm.b, negA.b], W=[g_fm.b])
                  psT = psum()
                  for blk in range(NB):
                      TR(psT.ap[:, blk * 8:(blk + 1) * 8], beta_fm.ap[0:8, blk * 128:(blk + 1) * 128], cmat.ap[0:8, 0, 0:8],
                         R=[beta_fm.b, cmat.b], W=[psT.b])
                      TR(psT.ap[:, 32 + blk * 8:32 + (blk + 1) * 8], g_fm.ap[0:8, blk * 128:(blk + 1) * 128],
                         cmat.ap[0:8, 0, 0:8], R=[g_fm.b, cmat.b], W=[psT.b])
                  CP(cols.ap[:, 0:64], psT.ap[:, 0:64], R=[psT.b], W=[cols.b])
                  psG = psum()
                  MM(psG.ap[:, 0:32], triu, GRAW, R=[cols.b, cmat.b], W=[psG.b])
                  CP(GCOL, psG.ap[:, 0:32], R=[psG.b], W=[cols.b])
                  MM(psG.ap[:, 32:64], e127, GCOL, R=[cols.b, cmat.b], W=[psG.b])
                  psF = psum()
                  for blk in range(NB):
                      MM(psF.ap[0:8, blk * 128:(blk + 1) * 128], cols.ap[:, 32 + blk * 8:32 + (blk + 1) * 8], triu,
                         R=[cols.b, cmat.b], W=[psF.b])
                  CP(G_fm.ap[0:8, :], psF.ap[0:8, :], R=[psF.b], W=[G_fm.b])
                  AC(BG, GCOL, AF.Exp, R=[cols.b], W=[cols.b])
                  TT(BG, BG, BETA, ALU.mult, R=[cols.b], W=[cols.b])
                  TT(TMP, psG.ap[:, 32:64], GCOL, ALU.subtract, R=[psG.b, cols.b], W=[cols.b])
                  AC(KDC, TMP, AF.Exp, R=[cols.b], W=[cols.b])
                  AC(EGL, psG.ap[:, 32:64], AF.Exp, R=[psG.b], W=[cols.b])
                  TS(NBETA, BETA, -1.0, ALU.mult, R=[cols.b], W=[cols.b])

                  if l == 0 and g == 0:
                      DBG(cols, cols.ap[:, 0:256])
                      DBG(G_fm, G_fm.ap[0:8, :])
                  _chk(2)
                  pre = a32t("pre", TG + 3)
                  acc = a32t("acc", TG)
                  sil = a32t("sil", TG)
                  sq = a32t("sq", TG)
                  rr = a32t("rr", TG)
                  pre_k = a32t("pre_k", TG + 3)
                  acc_k = a32t("acc_k", TG)
                  sil_k = a32t("sil_k", TG)
                  sq_k = a32t("sq_k", TG)
                  rr_k = a32t("rr_k", TG)
                  QKVZ = [[a32t16(f"qT{p}", TG), a32t16(f"kT{p}", TG), a32t(f"vT{p}", TG), a32t(f"zs{p}", TG)] for p in range(2)]
                  gam = a32t("gam", TG)
                  gmask = a32t("gmask", TG)
                  qds = [a32t16("qd0", TG), a32t16("qd1", TG)]
                  Sbf = a32t16("Sbf", H * 128)
                  identb = a32t16("identb", 128)
                  oT = a32t("oT", TG)
                  osq = a32t("osq", TG)
                  orr = a32t("orr", TG)
                  BT = []
                  for blk in range(NB):
                      d_ = {}
                      for nm in ("EL", "ET", "P0", "P1"):
                          d_[nm] = a32t(f"{nm}_{blk}", 128)
                      for i2 in range(2):
                          qq = a32t(f"QQ{i2}_{blk}", 256)
                          d_["QQ%d" % i2] = qq
                          d_["Q%d" % i2] = Tl(qq.ap[:, 0:128], qq.b)
                          d_["QT%d" % i2] = Tl(qq.ap[:, 128:256], qq.b)
                      for nm in ("kbg", "vbt", "Rb"):
                          d_[nm] = a32t16(f"{nm}_{blk}", 128)
                      BT.append(d_)
                  BTD = [[{"ATb": a32t16(f"ATb_{p}_{blk}", 128), "kdt": a32t16(f"kdt_{p}_{blk}", 128),
                           "ut": a32t(f"ut_{p}_{blk}", 128), "wTt": a32t16(f"wTt_{p}_{blk}", 128)} for blk in range(NB)]
                         for p in range(2)]
                  vnews = [a32t16("vnew0", 128), a32t16("vnew1", 128)]
                  CP(identb.ap, ident, R=[cmat.b], W=[identb.b])
                  AC(Sbf.ap, Sst.ap[:].rearrange("p h d -> p (h d)"), AF.Copy, R=[Sst.b], W=[Sbf.b])
                  BS = [slice(blk * 128, (blk + 1) * 128) for blk in range(NB)]

                  def conv_tile(ct, ps, pre, acc, l=l):
                      AC(pre.ap[:, 3:3 + TG], ps.ap[:, :], AF.Copy, R=[ps.b], W=[pre.b])
                      CP(pre.ap[:, 0:3], halo.ap[:, ct, :], R=[halo.b], W=[pre.b])
                      TS(acc.ap, pre.ap[:, 3:3 + TG], cw.ap[:, l, ct, 3:4], ALU.mult, R=[pre.b, cw.b], W=[acc.b])
                      for j in range(3):
                          STT(acc.ap, pre.ap[:, j:j + TG], cw.ap[:, l, ct, j:j + 1], acc.ap, ALU.mult, ALU.add,
                              R=[pre.b, cw.b, acc.b], W=[acc.b])
                      CP(halo.ap[:, ct, :], pre.ap[:, TG:TG + 3], R=[pre.b], W=[halo.b])

                  def FE(hd, l=l):
                      qT, kT, vT, zs = QKVZ[hd % 2]

                      def proj(which):
                          c0 = (C_QKV + (which * 8 + hd) * 128) if which < 3 else (C_Z + hd * 128)
                          wv, wt = WLOAD(wsrc(w_in, l, 0, KC, c0, 128), KC, 128)
                          ps = psum("fe")
                          proj_fm(wv, wt, 0, 128, ps.ap[:, :], ps)
                          return ps
                      pq = proj(0)
                      pk = proj(1)

                      def chain(which, dst, pp, T5):
                          pre_, acc_, sil_, sq_, rr_ = T5
                          conv_tile(which * 8 + hd, pp, pre_, acc_)
                          AC(sil_.ap, acc_.ap, AF.Silu, R=[acc_.b], W=[sil_.b])
                          AC(sq_.ap, sil_.ap, AF.Square, R=[sil_.b], W=[sq_.b])
                          psn = psum("fe")
                          MM(psn.ap[:, :], ones, sq_.ap, R=[sq_.b, cmat.b], W=[psn.b])
                          if which == 0:
                              AC(rr_.ap, psn.ap[:, :], AF.Sqrt, R=[psn.b, epsc.b], W=[rr_.b], bias=epsc.ap[:, 1:2], scale=128.0)
                          else:
                              AC(rr_.ap, psn.ap[:, :], AF.Sqrt, R=[psn.b, epsc.b], W=[rr_.b], bias=epsc.ap[:, 0:1], scale=1.0)
                          RCP(rr_.ap, rr_.ap, R=[rr_.b], W=[rr_.b])
                          TT(dst.ap, sil_.ap, rr_.ap, ALU.mult, R=[sil_.b, rr_.b], W=[dst.b])
                      outer = P.rec
                      P.rec = []
                      chain(0, qT, pq, (pre, acc, sil, sq, rr))
                      cq = P.rec
                      P.rec = []
                      chain(1, kT, pk, (pre_k, acc_k, sil_k, sq_k, rr_k))
                      ck = P.rec
                      P.rec = outer
                      for i_ in range(max(len(cq), len(ck))):
                          if i_ < len(cq):
                              outer.append(cq[i_])
                          if i_ < len(ck):
                              outer.append(ck[i_])
                      pv = proj(2)
                      pz = proj(3)
                      conv_tile(2 * 8 + hd, pv, pre, acc)
                      AC(vT.ap, acc.ap, AF.Silu, R=[acc.b], W=[vT.b])
                      AC(zs.ap, pz.ap[:, :], AF.Silu, R=[pz.b], W=[zs.b])

                  def CPS(hd):
                      qT, kT, vT, zs = QKVZ[hd % 2]
                      qd = qds[hd % 2]
                      BD = BTD[hd % 2]
                      CI = [blk * 8 + hd for blk in range(NB)]
                      psR = psb[7]
                      TS(gmask.ap[0:8, :], G_fm.ap[0:8, :], cmat.ap[0:8, 0, hd:hd + 1], ALU.mult, R=[G_fm.b, cmat.b], W=[gmask.b])
                      MM(psR.ap[:, :], cmat.ap[0:8, 1, :], gmask.ap[0:8, :], R=[cmat.b, gmask.b], W=[psR.b])
                      AC(gam.ap, psR.ap[:, :], AF.Exp, R=[psR.b], W=[gam.b])
                      TT(qd.ap, qT.ap, gam.ap, ALU.mult, R=[qT.b, gam.b], W=[qd.b])
                      for blk in range(NB):
                          t, d = BT[blk], BD[blk]
                          gcol = GCOL[:, CI[blk]:CI[blk] + 1]
                          STT(t["EL"].ap, psR.ap[:, BS[blk]], gcol, mnegL, ALU.subtract, ALU.subtract,
                              R=[psR.b, cols.b, cmat.b], W=[t["EL"].b])
                          STT(t["ET"].ap, psR.ap[:, BS[blk]], gcol, mnegT, ALU.subtract, ALU.add,
                              R=[psR.b, cols.b, cmat.b], W=[t["ET"].b])
                      for blk in range(NB):
                          t, d = BT[blk], BD[blk]
                          AC(t["EL"].ap, t["EL"].ap, AF.Exp, R=[t["EL"].b], W=[t["EL"].b], scale=-1.0)
                          AC(t["ET"].ap, t["ET"].ap, AF.Exp, R=[t["ET"].b], W=[t["ET"].b])
                      psKs = []
                      for blk in range(NB):
                          bs = BS[blk]
                          psK = psum("cp")
                          MM(psK.ap[:, 0:128], kT.ap[:, bs], kT.ap[:, bs], R=[kT.b], W=[psK.b])
                          MM(psK.ap[:, 128:256], kT.ap[:, bs], qT.ap[:, bs], R=[kT.b, qT.b], W=[psK.b])
                          psKs.append(psK)
                      for blk in range(NB):
                          t, d = BT[blk], BD[blk]
                          psK = psKs[blk]
                          STT(t["EL"].ap, psK.ap[:, 0:128], NBETA[:, CI[blk]:CI[blk] + 1], t["EL"].ap, ALU.mult, ALU.mult,
                              R=[psK.b, cols.b, t["EL"].b], W=[t["EL"].b])
                          TT(d["ATb"].ap, psK.ap[:, 128:256], t["ET"].ap, ALU.mult, R=[psK.b, t["ET"].b], W=[d["ATb"].b])
                      cur = []
                      for blk in range(NB):
                          t = BT[blk]
                          psN = psum("cp")
                          TR(psN.ap[:, 0:128], t["EL"].ap, ident, R=[t["EL"].b, cmat.b], W=[psN.b])
                          AC(t["Q0"].ap, psN.ap[:, 0:128], AF.Copy, R=[psN.b], W=[t["Q0"].b])
                          TT(t["P0"].ap, psN.ap[:, 0:128], ident, ALU.add, R=[psN.b, cmat.b], W=[t["P0"].b])
                          cur.append([t["Q0"], t["EL"], t["P0"]])
                      for m in range(1, 7):
                          banks = []
                          for blk in range(NB):
                              Q, QT, Pm = cur[blk]
                              pb = psum("cp")
                              if m < 6:
                                  MM(pb.ap[:, 0:128], QT.ap, Q.ap, R=[QT.b, Q.b], W=[pb.b])
                              MM(pb.ap[:, 128:256], Q.ap, QT.ap, R=[QT.b, Q.b], W=[pb.b])
                              banks.append(pb)
                          for blk in range(NB):
                              t = BT[blk]
                              pb = banks[blk]
                              QQn = t["QQ%d" % (m % 2)]
                              QTn = t["QT%d" % (m % 2)]
                              if m < 6:
                                  AC(QQn.ap, pb.ap[:, 0:256], AF.Copy, R=[pb.b], W=[QQn.b])
                              else:
                                  AC(QTn.ap, pb.ap[:, 128:256], AF.Copy, R=[pb.b], W=[QTn.b])
                          for blk in range(NB):
                              t = BT[blk]
                              pb = banks[blk]
                              Pm = cur[blk][2]
                              QTn = t["QT%d" % (m % 2)]
                              MM(pb.ap[:, 256:384], QTn.ap, Pm.ap, R=[QTn.b, Pm.b], W=[pb.b])
                          for blk in range(NB):
                              t = BT[blk]
                              pb = banks[blk]
                              Pm = cur[blk][2]
                              Pn = t["P%d" % (m % 2)] if m < 6 else t["Rb"]
                              TT(Pn.ap, pb.ap[:, 256:384], Pm.ap, ALU.add, R=[pb.b, Pm.b], W=[Pn.b])
                              cur[blk] = [t["Q%d" % (m % 2)], t["QT%d" % (m % 2)], Pn]
                      psXs = []
                      for blk in range(NB):
                          bs = BS[blk]
                          psX = psum("cp")
                          TR(psX.ap[:, 0:64].bitcast(BF16), kT.ap[:, bs], identb.ap, R=[kT.b, identb.b], W=[psX.b])
                          TR(psX.ap[:, 128:256], vT.ap[:, bs], ident, R=[vT.b, cmat.b], W=[psX.b])
                          psXs.append(psX)
                      for blk in range(NB):
                          t, d = BT[blk], BD[blk]
                          psX = psXs[blk]
                          ci = CI[blk]
                          AC(t["kbg"].ap, psX.ap[:, 0:64].bitcast(BF16), AF.Copy, R=[psX.b, cols.b], W=[t["kbg"].b], scale=BG[:, ci:ci + 1])
                          AC(d["kdt"].ap, psX.ap[:, 0:64].bitcast(BF16), AF.Copy, R=[psX.b, cols.b], W=[d["kdt"].b], scale=KDC[:, ci:ci + 1])
                          AC(t["vbt"].ap, psX.ap[:, 128:256], AF.Copy, R=[psX.b, cols.b], W=[t["vbt"].b], scale=BETA[:, ci:ci + 1])
                      psUs = []
                      for blk in range(NB):
                          t = BT[blk]
                          Rm = t["Rb"]
                          psU = psum("cp")
                          MM(psU.ap[:, 0:128], Rm.ap, t["vbt"].ap, R=[Rm.b, t["vbt"].b], W=[psU.b])
                          MM(psU.ap[:, 128:256], t["kbg"].ap, Rm.ap, R=[Rm.b, t["kbg"].b], W=[psU.b])
                          psUs.append(psU)
                      for blk in range(NB):
                          d = BD[blk]
                          psU = psUs[blk]
                          CP(d["ut"].ap, psU.ap[:, 0:128], R=[psU.b], W=[d["ut"].b])
                          CP(d["wTt"].ap, psU.ap[:, 128:256], R=[psU.b], W=[d["wTt"].b])

                  def REC(hd, l=l):
                      qT, kT, vT, zs = QKVZ[hd % 2]
                      qd = qds[hd % 2]
                      BD = BTD[hd % 2]
                      Sh = Sst.ap[:, hd, :]
                      Shb = Sbf.ap[:, hd * 128:(hd + 1) * 128]
                      for blk in range(NB):
                          d = BD[blk]
                          bs = BS[blk]
                          ci = blk * 8 + hd
                          vnew = vnews[blk % 2]
                          psa = psum("rec")
                          MM(psa.ap[:, 0:128], d["wTt"].ap, Shb, R=[d["wTt"].b, Sbf.b], W=[psa.b])
                          TT(vnew.ap, d["ut"].ap, psa.ap[:, 0:128], ALU.subtract, R=[d["ut"].b, psa.b], W=[vnew.b])
                          MM(psa.ap[:, 128:256], Shb, qd.ap[:, bs], start=True, stop=False, R=[Sbf.b, qd.b], W=[psa.b])
                          MM(psa.ap[:, 128:256], vnew.ap, d["ATb"].ap, start=False, stop=True, R=[vnew.b, d["ATb"].b], W=[psa.b])
                          MM(psa.ap[:, 256:384], d["kdt"].ap, vnew.ap, R=[d["kdt"].b, vnew.b], W=[psa.b])
                          STT(Sh, Sh, EGL[:, ci:ci + 1], psa.ap[:, 256:384], ALU.mult, ALU.add,
                              R=[Sst.b, cols.b, psa.b], W=[Sst.b])
                          AC(Shb, Sh, AF.Copy, R=[Sst.b], W=[Sbf.b])
                          AC(oT.ap[:, bs], psa.ap[:, 128:256], AF.Copy, R=[psa.b], W=[oT.b])
                      AC(osq.ap, oT.ap, AF.Square, R=[oT.b], W=[osq.b])
                      psn = psum("rec")
                      MM(psn.ap[:, :], ones, osq.ap, R=[osq.b, cmat.b], W=[psn.b])
                      AC(orr.ap, psn.ap[:, :], AF.Sqrt, R=[psn.b, epsc.b], W=[orr.b], bias=epsc.ap[:, 0:1], scale=1.0 / 128.0)
                      RCP(orr.ap, orr.ap, R=[orr.b], W=[orr.b])
                      STT(osq.ap, oT.ap, ong.ap[:, l:l + 1], orr.ap, ALU.mult, ALU.mult, R=[oT.b, ong.b, orr.b], W=[osq.b])
                      TT(ya.ap[:, hd, :], osq.ap, zs.ap, ALU.mult, R=[osq.b, zs.b], W=[ya.b])

                  P.begin()
                  FE(0)
                  P.play(P.end())
                  for k in range(H):
                      P.begin()
                      CPS(k)
                      sA = P.end()
                      sB = None
                      if k >= 1:
                          P.begin()
                          REC(k - 1)
                          sB = P.end()
                      sC = None
                      if k < H - 1:
                          P.begin()
                          FE(k + 1)
                          sC = P.end()
                      P.play(sA, sB, sC)
                  P.begin()
                  REC(H - 1)
                  P.play(P.end())

                  if l == 0 and g == 0:
                      DBG(ya, ya.ap[:, 0, :])
                  _chk(4)
                  P.barrier()
                  a32_reset()
                  sil = a32t("sil_g", TG)
                  gvs = [a32t(f"gv{i}", 1024) for i in range(NB)]
                  vg = a32t("vg", 1024)
                  lsm = a32t("lsm", 8)
                  lng = a32t("lng", 1024)
                  lnb = a32t("lnb", 1024)
                  LD(lng, lng.ap, lng_d[l])
                  LD(lnb, lnb.ap, lnb_d[l])
                  for half in range(2):
                      wv, wt = WLOAD(wsrc(w_in, l, 0, KC, C_V + half * 512, 512), KC, 512)
                      for ti in range(NB):
                          tsl = slice(ti * 128, (ti + 1) * 128)
                          ps = psum()
                          for kc in range(KC):
                              MM(ps.ap[:, :], h.ap[:, kc, tsl], wv.s(kc, 0, 512), start=(kc == 0), stop=(kc == KC - 1),
                                 R=[h.b, wt.b], W=[ps.b])
                          AC(gvs[ti].ap[:, half * 512:(half + 1) * 512], ps.ap[:, :], AF.Gelu, R=[ps.b], W=[gvs[ti].b])
                  for ti in range(NB):
                      tsl = slice(ti * 128, (ti + 1) * 128)
                      gv = gvs[ti]
                      s1, s2, mean, msq, var, rstd, nmr = [lsm.ap[:, i:i + 1] for i in range(7)]
                      AC(vg.ap, gv.ap, AF.Identity, R=[gv.b], W=[vg.b, lsm.b], accum=s1)
                      AC(vg.ap, gv.ap, AF.Square, R=[gv.b], W=[vg.b, lsm.b], accum=s2)
                      TS(mean, s1, 1.0 / 1024.0, ALU.mult, R=[lsm.b], W=[lsm.b])
                      TT(msq, mean, mean, ALU.mult, R=[lsm.b], W=[lsm.b])
                      STT(var, s2, 1.0 / 1024.0, msq, ALU.mult, ALU.subtract, R=[lsm.b], W=[lsm.b])
                      AC(var, var, AF.Sqrt, R=[lsm.b, epsc.b], W=[lsm.b], bias=epsc.ap[:, 0:1], scale=1.0)
                      RCP(rstd, var, R=[lsm.b], W=[lsm.b])
                      STT(nmr, mean, -1.0, rstd, ALU.mult, ALU.mult, R=[lsm.b], W=[lsm.b])
                      TS(vg.ap, gv.ap, rstd, ALU.mult, R=[gv.b, lsm.b], W=[vg.b], s2=nmr, op1=ALU.add)
                      TT(vg.ap, vg.ap, lng.ap, ALU.mult, R=[vg.b, lng.b], W=[vg.b])
                      TT(vg.ap, vg.ap, lnb.ap, ALU.add, R=[vg.b, lnb.b], W=[vg.b])
                      for gq in range(2):
                          ps = psum()
                          for j in range(4):
                              gr = gq * 4 + j
                              MM(ps.ap[:, j * 128:(j + 1) * 128], vg.ap[:, gr * 128:(gr + 1) * 128], wst.ap[:, gr, :],
                                 R=[vg.b, wst.b], W=[ps.b])
                          TT(yb.ap[:, gq * 4:gq * 4 + 4, tsl], ps.ap[:, :].rearrange("p (a b) -> p a b", a=4),
                             sgb.ap[:, gq * 4:gq * 4 + 4, :], ALU.add, R=[ps.b, sgb.b], W=[yb.b])
                  for gp in range(4):
                      wv, wt = WLOAD(wsrc(w_in, l, 0, KC, C_U + gp * 256, 256), KC, 256)
                      for j2 in range(2):
                          gr = gp * 2 + j2
                          ps = psum()
                          proj_fm(wv, wt, j2 * 128, 128, ps.ap[:, :], ps)
                          AC(sil.ap, ps.ap[:, :], AF.Gelu, R=[ps.b], W=[sil.b])
                          TT(yb.ap[:, gr, :], yb.ap[:, gr, :], sil.ap, ALU.mult, R=[yb.b, sil.b], W=[yb.b])

                  if l == 0 and g == 0:
                      DBG(yb, yb.ap[:, 0, :])
                  _chk(5)
                  P.barrier()
                  a32_reset()
                  s3 = a32t("s3", TG)
                  s4 = a32t("s4", TG)
                  t1 = a32t("t1", TG)
                  t2 = a32t("t2", TG)
                  for fp in range(KC // 2):
                      wva, wta = WLOAD(wsrc(w_ba, l, 0, 8, fp * 256, 256), 8, 256)
                      wvb, wtb = WLOAD(wsrc(w_bb, l, 0, 8, fp * 256, 256), 8, 256)
                      wvg, wtg = WLOAD(wsrc(w_in, l, 0, KC, C_GA + fp * 256, 256), KC, 256)
                      wvh, wth = WLOAD(wsrc(w_in, l, 0, KC, C_GB + fp * 256, 256), KC, 256)
                      for j2 in range(2):
                          f = fp * 2 + j2
                          c0 = j2 * 128
                          ps1 = psum()
                          proj_fm(wva, wta, c0, 128, ps1.ap[:, :], ps1, rhs_t=ya, nk=8)
                          ps2 = psum()
                          proj_fm(wvb, wtb, c0, 128, ps2.ap[:, :], ps2, rhs_t=yb, nk=8)
                          ps3 = psum()
                          proj_fm(wvg, wtg, c0, 128, ps3.ap[:, :], ps3)
                          ps4 = psum()
                          proj_fm(wvh, wth, c0, 128, ps4.ap[:, :], ps4)
                          AC(s3.ap, ps3.ap[:, :], AF.Sigmoid, R=[ps3.b], W=[s3.b])
                          AC(s4.ap, ps4.ap[:, :], AF.Sigmoid, R=[ps4.b], W=[s4.b])
                          TT(t1.ap, s3.ap, ps1.ap[:, :], ALU.mult, R=[s3.b, ps1.b], W=[t1.b])
                          TT(t2.ap, s4.ap, ps2.ap[:, :], ALU.mult, R=[s4.b, ps2.b], W=[t2.b])
                          TT(merged.ap[:, f, :], t1.ap, t2.ap, ALU.add, R=[t1.b, t2.b], W=[merged.b])
                  xo = [a32t("xo%d" % i_, 512) for i_ in range(8)]
                  xn = [a32t("xq%d" % i_, 512) for i_ in range(16)]
                  gB2 = a32t("gB2", D)
                  hnp = [a32t("hnp0", 512), a32t("hnp1", 512)]
                  njunk = a32t("njunk", 512)
                  ssqp = a32t("ssqp", 16)
                  nsms = [a32t("nsm2_%d" % i_, 8) for i_ in range(NB)]
                  LD(gB2, gB2.ap, gn_d[2 * l + 1])
                  for fo in range(4):
                      for ti in range(NB):
                          xot = xo[(fo % 2) * 4 + ti]
                          ob = outB[g * NB + ti][fo]
                          src = (x_in if l == 0 else out_d)[t0 + ti * 128:t0 + (ti + 1) * 128, fo * 512:(fo + 1) * 512]
                          LD(xot, xot.ap, src, R=([] if l == 0 else [ob]))
                      pso_ = [psum() for _ in range(NB)]
                      for (k0, nk) in ((0, 8), (8, 8)):
                          wv, wt = WLOAD(wsrc(w_out, l, k0 * 128, nk, fo * 512, 512), nk, 512)
                          for ti in range(NB):
                              tsl = slice(ti * 128, (ti + 1) * 128)
                              for kk in range(nk):
                                  f = k0 + kk
                                  MM(pso_[ti].ap[:, :], merged.ap[:, f, tsl], wv.s(kk, 0, 512), start=(f == 0), stop=(f == KC - 1),
                                     R=[merged.b, wt.b], W=[pso_[ti].b])
                      for ti in range(NB):
                          xot, xnt = xo[(fo % 2) * 4 + ti], xn[fo * 4 + ti]
                          ob = outB[g * NB + ti][fo]
                          TT(xnt.ap, xot.ap, pso_[ti].ap[:, :], ALU.add, R=[xot.b, pso_[ti].b], W=[xnt.b])
                          STO(xnt, xnt.ap, out_d[t0 + ti * 128:t0 + (ti + 1) * 128, fo * 512:(fo + 1) * 512], W=[ob])
                          AC(njunk.ap, xnt.ap, AF.Square, R=[xnt.b], W=[njunk.b, ssqp.b],
                             accum=ssqp.ap[:, ti * 4 + fo:ti * 4 + fo + 1])
                  cnt2 = 0
                  for ti in range(NB):
                      sm2 = nsms[ti]
                      c0 = ti * 4
                      ssq, rt, rstd = sm2.ap[:, 0:1], sm2.ap[:, 1:2], sm2.ap[:, 2:3]
                      TT(ssq, ssqp.ap[:, c0:c0 + 1], ssqp.ap[:, c0 + 1:c0 + 2], ALU.add, R=[ssqp.b], W=[sm2.b])
                      TT(ssq, ssq, ssqp.ap[:, c0 + 2:c0 + 3], ALU.add, R=[ssqp.b, sm2.b], W=[sm2.b])
                      TT(ssq, ssq, ssqp.ap[:, c0 + 3:c0 + 4], ALU.add, R=[ssqp.b, sm2.b], W=[sm2.b])
                      AC(rt, ssq, AF.Sqrt, R=[sm2.b, epsc.b], W=[sm2.b], bias=epsc.ap[:, 0:1], scale=1.0 / D)
                      RCP(rstd, rt, R=[sm2.b], W=[sm2.b])
                      for fo in range(4):
                          xnt = xn[fo * 4 + ti]
                          hn2 = hnp[cnt2 % 2]
                          STT(hn2.ap, xnt.ap, rstd, gB2.ap[:, fo * 512:(fo + 1) * 512], ALU.mult, ALU.mult,
                              R=[xnt.b, sm2.b, gB2.b], W=[hn2.b])
                          ps = psum()
                          for j in range(4):
                              TR(ps.ap[:, j * 128:(j + 1) * 128], hn2.ap[:, j * 128:(j + 1) * 128], ident,
                                 R=[hn2.b, cmat.b], W=[ps.b])
                          dst = h.ap[:, fo * 4:fo * 4 + 4, ti * 128:(ti + 1) * 128]
                          srcp = ps.ap[:, :].rearrange("p (a b) -> p a b", a=4)
                          if cnt2 % 2 == 0:
                              AC(dst, srcp, AF.Copy, R=[ps.b], W=[h.b])
                          else:
                              CP(dst, srcp, R=[ps.b], W=[h.b])
                          cnt2 += 1

                  if l == 0 and g == 0:
                      DBG(merged, merged.ap[:, 0, :])
                  _chk(6)
                  P.barrier()
                  a32_reset()
                  pre2 = a32t("pre2", TG + 2)
                  acc2 = a32t("acc2", TG)
                  sg2 = a32t("sg2", TG)
                  for ft in range(NFT):
                      if ft % 2 == 0:
                          wvg, wtg = WLOAD(wsrc(w_gate, l, 0, KC, ft * 128, 256), KC, 256)
                          wvu, wtu = WLOAD(wsrc(w_up, l, 0, KC, ft * 128, 256), KC, 256)
                      c0 = (ft % 2) * 128
                      psg = psum()
                      proj_fm(wvg, wtg, c0, 128, psg.ap[:, :], psg)
                      psu = psum()
                      proj_fm(wvu, wtu, c0, 128, psu.ap[:, :], psu)
                      AC(pre2.ap[:, 2:2 + TG], psg.ap[:, :], AF.Copy, R=[psg.b], W=[pre2.b])
                      CP(pre2.ap[:, 0:2], fhalo.ap[:, ft, :], R=[fhalo.b], W=[pre2.b])
                      TS(acc2.ap, pre2.ap[:, 2:2 + TG], fcw.ap[:, l, ft, 2:3], ALU.mult, R=[pre2.b, fcw.b], W=[acc2.b])
                      for j in range(2):
                          STT(acc2.ap, pre2.ap[:, j:j + TG], fcw.ap[:, l, ft, j:j + 1], acc2.ap, ALU.mult, ALU.add,
                              R=[pre2.b, fcw.b, acc2.b], W=[acc2.b])
                      CP(fhalo.ap[:, ft, :], pre2.ap[:, TG:TG + 2], R=[pre2.b], W=[fhalo.b])
                      AC(sg2.ap, acc2.ap, AF.Silu, R=[acc2.b, fcb.b], W=[sg2.b], bias=fcb.ap[:, l, ft:ft + 1])
                      TT(hidden.ap[:, ft, :], sg2.ap, psu.ap[:, :], ALU.mult, R=[sg2.b, psu.b], W=[hidden.b])
                  xo = [a32t("xo%d" % i_, 512) for i_ in range(8)]
                  xn = [a32t("xn%d" % i_, 512) for i_ in range(8)]
                  parts = [(0, 8), (8, 8), (16, 8), (24, 8), (32, 8), (40, 4)]
                  for fo in range(4):
                      for ti in range(NB):
                          xot = xo[(fo % 2) * 4 + ti]
                          ob = outB[g * NB + ti][fo]
                          dsl = out_d[t0 + ti * 128:t0 + (ti + 1) * 128, fo * 512:(fo + 1) * 512]
                          LD(xot, xot.ap, dsl, R=[ob])
                      pss_ = [psum() for _ in range(NB)]
                      for (k0, nk) in parts:
                          wv, wt = WLOAD(wsrc(w_down, l, k0 * 128, nk, fo * 512, 512), nk, 512)
                          for ti in range(NB):
                              tsl = slice(ti * 128, (ti + 1) * 128)
                              for kk in range(nk):
                                  kc = k0 + kk
                                  MM(pss_[ti].ap[:, :], hidden.ap[:, kc, tsl], wv.s(kk, 0, 512), start=(kc == 0), stop=(kc == NFT - 1),
                                     R=[hidden.b, wt.b], W=[pss_[ti].b])
                      for ti in range(NB):
                          xot, xnt = xo[(fo % 2) * 4 + ti], xn[(fo % 2) * 4 + ti]
                          ob = outB[g * NB + ti][fo]
                          dsl = out_d[t0 + ti * 128:t0 + (ti + 1) * 128, fo * 512:(fo + 1) * 512]
                          TT(xnt.ap, xot.ap, pss_[ti].ap[:, :], ALU.add, R=[xot.b, pss_[ti].b], W=[xnt.b])
                          STO(xnt, xnt.ap, dsl, W=[ob])
        except _Stop:
            pass
        P.barrier()
        a32_reset()
        gB = a32t("gBf", D)
        LD(gB, gB.ap, gn_d[2 * DEPTH])
        NBUF = 4
        xts = [a32t("fx%d" % i_, D) for i_ in range(NBUF)]
        hns = [a32t("fh%d" % i_, D) for i_ in range(NBUF)]
        sms = [a32t("fsm%d" % i_, 8) for i_ in range(NBUF)]
        ntile = ntg * NB

        def fload(ti):
            xt = xts[ti % NBUF]
            LD(xt, xt.ap, out_d[ti * 128:(ti + 1) * 128, :], R=list(outB[ti]))
        for ti in range(min(NBUF - 1, ntile)):
            fload(ti)
        for ti in range(ntile):
            if ti + NBUF - 1 < ntile:
                fload(ti + NBUF - 1)
            xt, hn, sm = xts[ti % NBUF], hns[ti % NBUF], sms[ti % NBUF]
            ssq, rt, rstd = sm.ap[:, 0:1], sm.ap[:, 1:2], sm.ap[:, 2:3]
            AC(hn.ap, xt.ap, AF.Square, R=[xt.b], W=[hn.b, sm.b], accum=ssq)
            AC(rt, ssq, AF.Sqrt, R=[sm.b, epsc.b], W=[sm.b], bias=epsc.ap[:, 0:1], scale=1.0 / D)
            RCP(rstd, rt, R=[sm.b], W=[sm.b])
            STT(hn.ap, xt.ap, rstd, gB.ap, ALU.mult, ALU.mult, R=[xt.b, sm.b, gB.b], W=[hn.b])
            STO(hn, hn.ap, out_d[ti * 128:(ti + 1) * 128, :], W=list(outB[ti]), final=True)
        P.emit()
    return nc


def host_consts():
    i = np.arange(128)
    ident = np.eye(128, dtype=np.float32)
    ones = np.ones((128, 128), np.float32)
    triu = (i[:, None] <= i[None, :]).astype(np.float32)
    e127 = np.zeros((128, 128), np.float32)
    e127[127, :] = 1.0
    NEG = np.float32(-1e30)
    mnegL = np.where(i[:, None] > i[None, :], 0.0, NEG).astype(np.float32)
    mnegT = np.where(i[None, :] >= i[:, None], 0.0, NEG).astype(np.float32)
    cmat = np.stack([ident, ones, triu, e127, mnegL, mnegT], axis=1)
    sel = np.zeros((8, 8, 128), np.float32)
    for hh in range(8):
        sel[hh, hh, :] = 1.0
    return np.ascontiguousarray(cmat), sel


def prep_shared(inp):
    f = lambda a: np.ascontiguousarray(np.asarray(a, dtype=np.float32))
    cmat, sel = host_consts()
    rep = lambda v: np.broadcast_to(np.asarray(v, np.float32)[None, :], (128, v.shape[-1]))
    gn = np.stack([rep(inp["norm1_g"][0]), rep(inp["norm2_g"][0]), rep(inp["norm1_g"][1]), rep(inp["norm2_g"][1]),
                   rep(inp["final_norm_g"])], axis=0)
    cw = np.asarray(inp["dn_conv_w"], np.float32).reshape(DEPTH, 4, 24, 128).transpose(3, 0, 2, 1)
    fcw = np.asarray(inp["ffn_conv_w"], np.float32).reshape(DEPTH, 3, NFT, 128).transpose(3, 0, 2, 1)
    fcb = np.asarray(inp["ffn_conv_b"], np.float32).reshape(DEPTH, NFT, 128).transpose(2, 0, 1)
    ong = np.asarray(inp["dn_onorm_g"], np.float32).T
    alog = np.asarray(inp["dn_a_log"], np.float32).T
    dtb = np.asarray(inp["dn_dt_bias"], np.float32).T
    lng = np.stack([rep(inp["sg_ln_g"][l]) for l in range(DEPTH)], 0)
    lnb = np.stack([rep(inp["sg_ln_b"][l]) for l in range(DEPTH)], 0)
    wst = np.asarray(inp["sg_w"], np.float32).transpose(0, 3, 1, 2)
    sgb = np.broadcast_to(np.asarray(inp["sg_b"], np.float32)[:, None, :, :], (DEPTH, 128, 8, 128))
    shared = {
        "w_in": f(inp["w_in"]), "w_branch_a": f(inp["w_branch_a"]), "w_branch_b": f(inp["w_branch_b"]),
        "w_out": f(inp["w_out"]), "ffn_w_gate": f(inp["ffn_w_gate"]), "ffn_w_up": f(inp["ffn_w_up"]),
        "ffn_w_down": f(inp["ffn_w_down"]), "cmat": cmat, "sel": sel, "gnorm": f(gn), "cw": f(cw), "fcw": f(fcw),
        "fcb": f(fcb), "ong": f(ong), "alog": f(alog), "dtb": f(dtb), "lng": f(lng), "lnb": f(lnb),
        "wst": f(wst), "sgb": f(sgb),
    }
    return shared


def kernel(**inputs):
    x = np.asarray(inputs["x"], dtype=np.float32)
    shared = prep_shared(inputs)
    nc = build_program()
    in_maps = []
    for b in range(BATCH):
        m = dict(shared)
        m["x"] = np.ascontiguousarray(x[b])
        in_maps.append(m)
    res = run_bass_kernel_spmd(nc, in_maps, core_ids=list(range(BATCH)))
    out = np.stack([np.asarray(res.results[b]["out"], dtype=np.float32) for b in range(BATCH)], axis=0)
    return out
```
